# Optimizing a Trainium2 kernel written in Bass

```python
import jax, jax.numpy as jnp
from jax import lax
import numpy as np

D_MODEL = 1024
BATCH = 8
SEQ = 4096
DEPTH = 2

LRU_HEADS = 16
LRU_HEAD_DIM = 64
D_LRU = LRU_HEADS * LRU_HEAD_DIM
SC_GROUPS = 8
SC_GROUP_DIM = 64
D_SC = SC_GROUPS * SC_GROUP_DIM
D_MIX = D_LRU + D_SC
D_IN = 2 * D_LRU + 3 * D_SC
LRU_CONV_WIDTH = 4
SC_CONV_WIDTH = 3
RG_C = 8.0
D_FF = 3 * D_MODEL
FFN_CONV_WIDTH = 3
EPS = 1e-6

kernel_name = "hymba_style_rglru_shortconv_convffn"


def rms_norm(x, g):
    xf = x.astype(jnp.float32)
    y = xf * lax.rsqrt(jnp.mean(xf * xf, axis=-1, keepdims=True) + EPS)
    return (y * g.astype(jnp.float32)).astype(x.dtype)


def causal_dwconv(x, w):
    k_width = w.shape[0]
    s = x.shape[1]
    xp = jnp.pad(x, ((0, 0), (k_width - 1, 0), (0, 0)))
    y = xp[:, 0:s] * w[0]
    for k in range(1, k_width):
        y = y + xp[:, k:k + s] * w[k]
    return y


def rg_lru(x, wa, ba, wx, bx, lam):
    bsz, s, c = x.shape
    xh = x.reshape(bsz, s, LRU_HEADS, LRU_HEAD_DIM)
    r = jax.nn.sigmoid(jnp.einsum('bshi,hij->bshj', xh, wa).reshape(bsz, s, c) + ba)
    i = jax.nn.sigmoid(jnp.einsum('bshi,hij->bshj', xh, wx).reshape(bsz, s, c) + bx)
    log_a = -RG_C * r.astype(jnp.float32) * jax.nn.softplus(-lam.astype(jnp.float32))
    a = jnp.exp(log_a)
    mult = jnp.sqrt(-jnp.expm1(2.0 * log_a))
    b = mult * (i * x).astype(jnp.float32)

    def combine(left, right):
        a1, b1 = left
        a2, b2 = right
        return a1 * a2, a2 * b1 + b2

    _, h = lax.associative_scan(combine, (a, b), axis=1)
    return h.astype(x.dtype)


def setup_inputs(seed: int = 0) -> dict:
    key = jax.random.key(seed)
    ks = jax.random.split(key, 20)
    f32 = jnp.float32
    res_scale = (2.0 * DEPTH) ** -0.5
    x = jax.random.normal(ks[0], (BATCH, SEQ, D_MODEL), f32)
    norm1_g = 1.0 + 0.02 * jax.random.normal(ks[1], (DEPTH, D_MODEL), f32)
    w_in = jax.random.normal(ks[2], (DEPTH, D_MODEL, D_IN), f32) * D_MODEL ** -0.5
    lru_conv_w = jax.random.normal(ks[3], (DEPTH, LRU_CONV_WIDTH, D_LRU), f32) * LRU_CONV_WIDTH ** -0.5
    lru_conv_b = 0.02 * jax.random.normal(ks[4], (DEPTH, D_LRU), f32)
    lru_wa = jax.random.normal(ks[5], (DEPTH, LRU_HEADS, LRU_HEAD_DIM, LRU_HEAD_DIM), f32) * LRU_HEAD_DIM ** -0.5
    lru_ba = 0.02 * jax.random.normal(ks[6], (DEPTH, D_LRU), f32)
    lru_wx = jax.random.normal(ks[7], (DEPTH, LRU_HEADS, LRU_HEAD_DIM, LRU_HEAD_DIM), f32) * LRU_HEAD_DIM ** -0.5
    lru_bx = 0.02 * jax.random.normal(ks[8], (DEPTH, D_LRU), f32)
    u = jax.random.uniform(ks[9], (DEPTH, D_LRU), f32, minval=0.9, maxval=0.999)
    a0 = u ** (1.0 / RG_C)
    lru_lambda = jnp.log(a0) - jnp.log1p(-a0)
    sc_conv_w = jax.random.normal(ks[10], (DEPTH, SC_CONV_WIDTH, D_SC), f32) * SC_CONV_WIDTH ** -0.5
    w_out = jax.random.normal(ks[11], (DEPTH, D_MIX, D_MODEL), f32) * D_MIX ** -0.5 * res_scale
    norm2_g = 1.0 + 0.02 * jax.random.normal(ks[12], (DEPTH, D_MODEL), f32)
    w_up = jax.random.normal(ks[13], (DEPTH, D_MODEL, 2 * D_FF), f32) * D_MODEL ** -0.5
    ffn_conv_w = jax.random.normal(ks[14], (DEPTH, FFN_CONV_WIDTH, 2 * D_FF), f32) * FFN_CONV_WIDTH ** -0.5
    w_down = jax.random.normal(ks[15], (DEPTH, D_FF, D_MODEL), f32) * D_FF ** -0.5 * res_scale
    final_g = 1.0 + 0.02 * jax.random.normal(ks[16], (D_MODEL,), f32)
    return {"x": x, "norm1_g": norm1_g, "w_in": w_in, "lru_conv_w": lru_conv_w,
            "lru_conv_b": lru_conv_b, "lru_wa": lru_wa, "lru_ba": lru_ba, "lru_wx": lru_wx,
            "lru_bx": lru_bx, "lru_lambda": lru_lambda, "sc_conv_w": sc_conv_w, "w_out": w_out,
            "norm2_g": norm2_g, "w_up": w_up, "ffn_conv_w": ffn_conv_w, "w_down": w_down,
            "final_g": final_g}


def reference(x, norm1_g, w_in, lru_conv_w, lru_conv_b, lru_wa, lru_ba, lru_wx, lru_bx,
              lru_lambda, sc_conv_w, w_out, norm2_g, w_up, ffn_conv_w, w_down, final_g):
    splits = [D_LRU, 2 * D_LRU, 2 * D_LRU + D_SC, 2 * D_LRU + 2 * D_SC]
    for l in range(DEPTH):
        h = rms_norm(x, norm1_g[l])
        z = jnp.einsum('bsd,de->bse', h, w_in[l])
        lru_x, lru_gate, sc_b, sc_c, sc_x = jnp.split(z, splits, axis=-1)
        lru_x = causal_dwconv(lru_x, lru_conv_w[l]) + lru_conv_b[l]
        y_lru = rg_lru(lru_x, lru_wa[l], lru_ba[l], lru_wx[l], lru_bx[l], lru_lambda[l]) \
            * jax.nn.gelu(lru_gate)
        y_sc = sc_b * causal_dwconv(sc_c * sc_x, sc_conv_w[l])
        y_mix = jnp.concatenate([y_lru, y_sc], axis=-1)
        x = x + jnp.einsum('bse,ed->bsd', y_mix, w_out[l])
        h = rms_norm(x, norm2_g[l])
        u = causal_dwconv(jnp.einsum('bsd,df->bsf', h, w_up[l]), ffn_conv_w[l])
        gate, up = jnp.split(u, 2, axis=-1)
        x = x + jnp.einsum('bsf,fd->bsd', jax.nn.gelu(gate) * up, w_down[l])
    return rms_norm(x, final_g)
```

```python
import numpy as np
from contextlib import ExitStack

import concourse.bass as bass
import concourse.mybir as mybir
from concourse.bass_utils import run_bass_kernel_spmd

F32 = mybir.dt.float32
BF16 = mybir.dt.bfloat16
AF = mybir.ActivationFunctionType
ALU = mybir.AluOpType

D = 1024
KD = D // 128
DEPTH = 2
SEQ = 4096
NCORES = 8
D_LRU = 1024
D_SC = 512
D_IN = 3584
D_FF = 3072
EPS = 1e-6
N_IN = D_IN // 128
N_UP = 2 * D_FF // 128
N_FF = D_FF // 128
N_MIX = (D_LRU + D_SC) // 128
HALO = 3
WMAX = 456
NSLOT = 13
SAME_ENGINE_SYNC = True
FUSE_WAITS = True
SAME_ENGINE_MIN = 200

V_G1 = 0
V_CW = 8
V_CB = 40
V_BA = 48
V_BX = 56
V_LAM = 64
V_SCW = 72
V_G2 = 84
V_FCW = 92
NVL = 92 + 144
DV_HBA = 0
DV_HBX = 8
DV_C = 16
DV_HC = 24
NDV_BASE = 32
NDV = 32 + 40


class Sched:
    def __init__(self, nc, es):
        self.nc = nc
        self.es = es
        self.eng = {"pe": nc.tensor, "act": nc.scalar, "dve": nc.vector,
                    "pool": nc.gpsimd, "sp": nc.sync}
        self.compute = ("pe", "act", "dve", "pool")
        self.esem = {e: es.enter_context(nc.semaphore("sem_" + e)) for e in self.compute}
        self.cnt = {e: 0 for e in self.compute}
        self.dsem = {}
        self.waited = {e: {} for e in self.eng}
        self.recs = {}
        self.nwaits = 0

    def _sem(self, name):
        if name.startswith("E:"):
            return self.esem[name[2:]]
        return self.dsem[name][0]

    def dma_sem(self, name):
        if name not in self.dsem:
            self.dsem[name] = [self.es.enter_context(self.nc.semaphore("dsem_" + name)), 0]
        return name

    def _deps(self, e, reads, writes):
        deps = {}

        def add(rec):
            if rec[5] == e and e in self.compute and not SAME_ENGINE_SYNC and rec[1] - rec[0] >= SAME_ENGINE_MIN:
                return
            if deps.get(rec[3], 0) < rec[4]:
                deps[rec[3]] = rec[4]

        for (buf, lo, hi) in reads:
            for rec in self.recs.get(buf, ()):
                if rec[2] and rec[0] < hi and lo < rec[1]:
                    add(rec)
        for (buf, lo, hi) in writes:
            for rec in self.recs.get(buf, ()):
                if rec[0] < hi and lo < rec[1]:
                    add(rec)
        return deps

    def _wait(self, e, deps, keep_last=False):
        w = self.waited[e]
        need = [(name, val) for name, val in deps.items() if w.get(name, 0) < val]
        last = None
        if keep_last and need and FUSE_WAITS:
            last = need.pop()
        for name, val in need:
            self.eng[e].wait_ge(self._sem(name), val)
            w[name] = val
            self.nwaits += 1
        if last is not None:
            w[last[0]] = last[1]
        return last

    def _fuse(self, ins, last):
        if last is not None:
            ins._wait_ge(self._sem(last[0]), last[1])

    def _record(self, e, reads, writes, name, val):
        for (buf, lo, hi) in writes:
            lst = self.recs.setdefault(buf, [])
            lst[:] = [r for r in lst if not (lo <= r[0] and r[1] <= hi)]
            lst.append([lo, hi, True, name, val, e])
        for (buf, lo, hi) in reads:
            lst = self.recs.setdefault(buf, [])
            lst[:] = [r for r in lst if not ((not r[2]) and r[5] == e and lo <= r[0] and r[1] <= hi)]
            lst.append([lo, hi, False, name, val, e])

    def op(self, e, fn, reads=(), writes=()):
        last = self._wait(e, self._deps(e, reads, writes), keep_last=True)
        ins = fn(self.eng[e])
        self._fuse(ins, last)
        self.cnt[e] += 1
        ins.then_inc(self.esem[e], 1)
        self._record(e, reads, writes, "E:" + e, self.cnt[e])

    def mm_group(self, out_region, mms):
        e = "pe"
        ticket = self.cnt[e] + 1
        bank_wait = self._wait(e, self._deps(e, (), (out_region,)), keep_last=True)
        last = None
        for fn, reads in mms:
            self._wait(e, self._deps(e, reads, ()))
            last = fn(self.eng[e])
            if bank_wait is not None:
                self._fuse(last, bank_wait)
                bank_wait = None
            self._record(e, reads, (), "E:pe", ticket)
        last.then_inc(self.esem[e], 1)
        self.cnt[e] = ticket
        self._record(e, (), (out_region,), "E:pe", ticket)

    def mm_interleaved(self, bank_regions, seq):
        e = "pe"
        opened = set()
        pending = []
        for fn, reads, bi, last in seq:
            if bi not in opened:
                opened.add(bi)
                self._wait(e, self._deps(e, (), (bank_regions[bi],)))
            self._wait(e, self._deps(e, reads, ()))
            ins = fn(self.eng[e])
            if last:
                self.cnt[e] += 1
                ins.then_inc(self.esem[e], 1)
                t = self.cnt[e]
                for r in pending:
                    self._record(e, r, (), "E:pe", t)
                pending = []
                self._record(e, reads, (), "E:pe", t)
                self._record(e, (), (bank_regions[bi],), "E:pe", t)
            else:
                pending.append(reads)
        assert not pending

    def dma(self, e, semname, fns, reads=(), writes=()):
        if not isinstance(fns, (list, tuple)):
            fns = [fns]
        self._wait(e, self._deps(e, reads, writes))
        h = self.dsem[semname]
        for fn in fns:
            fn(self.eng[e]).then_inc(h[0], 16)
            h[1] += 16
        self._record("dma:" + semname, reads, writes, semname, h[1])

    def wait_dma_total(self, e, semname):
        h = self.dsem[semname]
        if self.waited[e].get(semname, 0) < h[1]:
            self.eng[e].wait_ge(h[0], h[1])
            self.waited[e][semname] = h[1]


def build_program(S=SEQ, depth=DEPTH, W=WMAX, nslot=NSLOT, debug=False):
    nc = bass.Bass("TRN2", target_bir_lowering=False)
    dbg_names = []
    if debug:
        dbg_f = nc.dram_tensor("dbg_f", [64, 128, 512], F32, kind="ExternalOutput").ap()
        dbg_b = nc.dram_tensor("dbg_b", [64, 128, 512], BF16, kind="ExternalOutput").ap()
    tiles = [(s0, min(W, S - s0)) for s0 in range(0, S, W)]
    WM = W
    debug_tile = (debug - 1) if debug else -1
    NV = depth * NVL + 8

    xT = nc.dram_tensor("xT", [KD, 128, S], F32, kind="ExternalInput").ap()
    win_f = nc.dram_tensor("win_t", [depth * N_IN * 128, 1024], F32, kind="ExternalInput").ap()
    wup_f = nc.dram_tensor("wup_t", [depth * N_UP * 128, 1024], F32, kind="ExternalInput").ap()
    wout_f = nc.dram_tensor("wout_t", [depth * KD * 128, 1536], F32, kind="ExternalInput").ap()
    wdn_f = nc.dram_tensor("wdn_t", [depth * KD * 128 * 2, 1536], F32, kind="ExternalInput").ap()
    bd_f = nc.dram_tensor("bd_t", [128, depth * 16 * 128], F32, kind="ExternalInput").ap()
    vecs = nc.dram_tensor("vecs", [128, NV], F32, kind="ExternalInput").ap()
    outT = nc.dram_tensor("outT", [KD, 128, S], F32, kind="ExternalOutput").ap()
    win_b = nc.dram_tensor("win_b", [depth * N_IN * 128, 1024], BF16, kind="Internal").ap()
    wup_b = nc.dram_tensor("wup_b", [depth * N_UP * 128, 1024], BF16, kind="Internal").ap()
    wout_b = nc.dram_tensor("wout_b", [depth * KD * 128, 1536], BF16, kind="Internal").ap()
    wdn_b = nc.dram_tensor("wdn_b", [depth * KD * 128 * 2, 1536], BF16, kind="Internal").ap()

    with ExitStack() as es:
        E = es.enter_context
        sb = lambda name, shape, dt: E(nc.sbuf_tensor(name, shape, dt))
        X = [sb("X0", [128, KD, WM], F32), sb("X1", [128, KD, WM], F32)]
        H = sb("H", [128, KD, WM + HALO], BF16)
        HCAR = sb("HCAR", [128, depth * 2, KD, HALO], BF16)
        SQ = sb("SQ", [128, KD, WM], BF16)
        DUM = sb("DUM", [128, 2], F32)
        Y = sb("Y", [128, N_MIX, WM], BF16)
        Q = sb("Q", [128, KD, WM], F32)
        A = sb("A", [128, KD, WM], F32)
        M = sb("M", [128, KD, WM], F32)
        T = sb("T", [128, 4, WM], F32)
        XCB = sb("XCB", [128, 4, WM], BF16)
        TH = sb("TH", [128, 2, WM], F32)
        HS = sb("HS", [128, 2, WM], F32)
        CS = sb("CS", [128, 2, WM + 2], F32)
        CT = sb("CT", [128, 2, WM], F32)
        ACTB = sb("ACTB", [128, N_FF, WM], BF16)
        G = sb("G", [128, 2, WM], F32)
        U = sb("U", [128, 2, WM], F32)
        GE = sb("GE", [128, 2, WM], F32)
        V = sb("V", [128, NV], F32)
        DV = sb("DV", [128, depth * NDV], F32)
        HST = sb("HST", [128, depth * KD], F32)
        ONES = sb("ONES", [128, 128], BF16)
        ZEROS = sb("ZEROS", [128, 128], BF16)
        BD = sb("BD", [128, depth * 16, 128], BF16)
        WR = sb("WR", [128, nslot, 1024], BF16)
        PS = [E(nc.psum_tensor("PS%d" % i, [128, 512], F32)) for i in range(8)]

        sc = Sched(nc, es)
        for i in range(nslot):
            sc.dma_sem("w%d" % i)
        for n in ("x0", "x1", "o0", "o1", "const"):
            sc.dma_sem(n)

        sc.dma_sem("dbg")

        def dump(name, ap, region, n, bf=False):
            if not debug:
                return
            idx = len(dbg_names)
            dbg_names.append((name, bf, n))
            dst = (dbg_b if bf else dbg_f)[idx, :, 0:n]
            sc.dma("sp", "dbg", lambda q: q.dma_start(out=dst, in_=ap), reads=[region])
            sc.wait_dma_total("sp", "dbg")

        def r3(name, L, k, a, b):
            return (name, k * L + a, k * L + b)

        def rX(xi, k, a, b):
            return r3("X%d" % xi, WM, k, a, b)

        def rH(k, a, b):
            return r3("H", WM + HALO, k, a, b)

        rot = {}

        def nxt(name, n=2):
            rot[name] = (rot.get(name, -1) + 1) % n
            return rot[name]

        bank_ctr = [0]

        def new_bank():
            b = bank_ctr[0] % 8
            bank_ctr[0] += 1
            return b

        def rP(b):
            return ("PS%d" % b, 0, 512)

        s0_, w_ = tiles[0]
        sc.dma("sp", "x0",
               lambda q: q.dma_start(out=X[0][:, :, 0:w_], in_=xT[:, :, s0_:s0_ + w_].rearrange("k p s -> p k s")),
               writes=[rX(0, k, 0, w_) for k in range(KD)])
        sc.dma("sp", "const", lambda q: q.dma_start(out=V[:], in_=vecs), writes=[("V", 0, NV)])
        for l in range(depth):
            sc.dma("pool", sc.dma_sem("bd%d" % l),
                   lambda q, l=l: q.dma_start(out=BD[:, l * 16:(l + 1) * 16, :],
                                              in_=bd_f[:, l * 2048:(l + 1) * 2048].rearrange("p (a b) -> p a b", b=128)),
                   writes=[("BD", l * 16 * 128, (l + 1) * 16 * 128)])
        EPSC = sb("EPSC", [128, 2], F32)
        sc.op("dve", lambda v: v.memset(EPSC[:], EPS), writes=[("EPSC", 0, 2)])
        sc.op("dve", lambda v: v.memset(DUM[:], 1.0), writes=[("DUM", 0, 2)])
        sc.op("dve", lambda v: v.memset(ONES[:], 1.0), writes=[("ONES", 0, 128)])
        sc.op("dve", lambda v: v.memset(ZEROS[:], 0.0), writes=[("ZEROS", 0, 128)])
        sc.op("dve", lambda v: v.memset(HST[:], 0.0), writes=[("HST", 0, depth * KD)])
        sc.op("pool", lambda g: g.memset(HCAR[:], 0.0),
              writes=[("HCAR", 0, depth * 2 * KD * HALO)])
        for l in range(depth):
            vb = l * NVL
            db = l * NDV
            lam = V[:, vb + V_LAM:vb + V_LAM + 8]
            tcol = lambda i: DV[:, db + NDV_BASE + 8 * i:db + NDV_BASE + 8 * i + 8]
            rD = ("DV", db, db + NDV)
            rVv = ("V", 0, NV)
            dv = lambda fn, rd=(rD,): sc.op("dve", fn, reads=list(rd), writes=[rD])
            sc.op("act", lambda a: a.activation(out=tcol(0), in_=lam, func=AF.Abs), reads=[rVv], writes=[rD])
            sc.op("act", lambda a: a.activation(out=tcol(1), in_=tcol(0), func=AF.Exp, scale=-1.0),
                  reads=[rD], writes=[rD])
            dv(lambda v: v.tensor_scalar(out=tcol(2), in0=tcol(1), scalar1=2.0, scalar2=None, op0=ALU.add))
            dv(lambda v: v.reciprocal(out=tcol(2), in_=tcol(2)))
            dv(lambda v: v.tensor_tensor(out=tcol(2), in0=tcol(2), in1=tcol(1), op=ALU.mult))
            dv(lambda v: v.tensor_tensor(out=tcol(3), in0=tcol(2), in1=tcol(2), op=ALU.mult))
            dv(lambda v: v.tensor_scalar(out=tcol(4), in0=tcol(3), scalar1=1.0 / 11.0, scalar2=1.0 / 9.0,
                                         op0=ALU.mult, op1=ALU.add))
            for cst in (1.0 / 7.0, 1.0 / 5.0, 1.0 / 3.0, 1.0):
                dv(lambda v: v.tensor_tensor(out=tcol(4), in0=tcol(4), in1=tcol(3), op=ALU.mult))
                dv(lambda v, cst=cst: v.tensor_scalar(out=tcol(4), in0=tcol(4), scalar1=cst, scalar2=None, op0=ALU.add))
            dv(lambda v: v.tensor_tensor(out=tcol(4), in0=tcol(4), in1=tcol(2), op=ALU.mult))
            dv(lambda v: v.tensor_scalar(out=tcol(0), in0=lam, scalar1=-1.0, scalar2=0.0, op0=ALU.mult, op1=ALU.max),
               rd=(rVv, rD))
            dv(lambda v: v.scalar_tensor_tensor(out=tcol(0), in0=tcol(4), scalar=2.0, in1=tcol(0),
                                                op0=ALU.mult, op1=ALU.add))
            dv(lambda v: v.tensor_scalar(out=DV[:, db + DV_C:db + DV_C + 8], in0=tcol(0), scalar1=-8.0, scalar2=None,
                                         op0=ALU.mult))
            dv(lambda v: v.tensor_scalar(out=DV[:, db + DV_HC:db + DV_HC + 8], in0=tcol(0), scalar1=-4.0, scalar2=None,
                                         op0=ALU.mult))
            sc.op("dve", lambda v, vb=vb, db=db: v.tensor_scalar(
                out=DV[:, db + DV_HBA:db + DV_HBA + 8], in0=V[:, vb + V_BA:vb + V_BA + 8],
                scalar1=0.5, scalar2=None, op0=ALU.mult),
                reads=[("V", 0, NV)], writes=[("DV", db + DV_HBA, db + DV_HBA + 8)])
            sc.op("dve", lambda v, vb=vb, db=db: v.tensor_scalar(
                out=DV[:, db + DV_HBX:db + DV_HBX + 8], in0=V[:, vb + V_BX:vb + V_BX + 8],
                scalar1=0.5, scalar2=None, op0=ALU.mult),
                reads=[("V", 0, NV)], writes=[("DV", db + DV_HBX, db + DV_HBX + 8)])

        dump("DV", DV[:, 0:NDV], ("DV", 0, NDV), NDV)

        def weight_order():
            for (s0, w) in tiles:
                for l in range(depth):
                    for j in range(8 + SK):
                        if j < 8:
                            yield ("in", l, j, 0)
                        if j in SC_AT:
                            for base in (16, 20, 24):
                                yield ("in", l, base + SC_AT[j], 0)
                    for j in range(8):
                        yield ("in", l, 8 + j, 0)
                    for m in range(8):
                        yield ("out", l, m, 0)
                        yield ("out", l, m, 1)
                    for j in range(N_FF):
                        yield ("up", l, j, 0)
                        yield ("up", l, N_FF + j, 0)
                    for m in range(8):
                        for part in range(3):
                            yield ("dn", l, m, part)

        SK = 3
        SC_AT = {2: 0, 4: 1, 6: 2, 9: 3}
        worder = list(weight_order())
        wstate = {"issued": 0, "pos": 0}

        F32T = {"win": win_f, "wup": wup_f, "wout": wout_f, "wdn": wdn_f}
        B16T = {"win": win_b, "wup": wup_b, "wout": wout_b, "wdn": wdn_b}

        def w_pieces(key):
            kind, l, idx, part = key
            if kind == "in":
                return [("win", (l * N_IN + idx) * 128, 0, 1024, 0)]
            if kind == "up":
                return [("wup", (l * N_UP + idx) * 128, 0, 1024, 0)]
            if kind == "out":
                r = (l * KD + idx) * 128
                return [("wout", r, 0, 1024, 0)] if part == 0 else [("wout", r, 1024, 512, 0)]
            r = (l * KD + idx) * 256
            if part == 0:
                return [("wdn", r, 0, 1024, 0)]
            if part == 1:
                return [("wdn", r, 1024, 512, 0), ("wdn", r + 128, 0, 512, 512)]
            return [("wdn", r + 128, 512, 1024, 0)]

        def dreg(pc):
            name, r, c0, n, d = pc
            return ("%s_b:%d" % (name, c0), r, r + 128)

        def issue_load(i):
            slot = i % nslot
            pcs = w_pieces(worder[i])
            ntot = sum(pc[3] for pc in pcs)
            sc.dma("sp", "w%d" % slot,
                   [lambda q, pc=pc: q.dma_start(out=WR[:, slot, pc[4]:pc[4] + pc[3]],
                                                 in_=B16T[pc[0]][pc[1]:pc[1] + 128, pc[2]:pc[2] + pc[3]]) for pc in pcs],
                   reads=[dreg(pc) for pc in pcs],
                   writes=[("WR", slot * 1024, slot * 1024 + ntot)])

        T0N = len(worder) // len(tiles)
        NSTG = 4
        CA = 2
        STG = sb("STG", [128, NSTG, 1024], F32)
        for i_ in range(NSTG):
            sc.dma_sem("stg%d" % i_)
        for i_ in range(nslot):
            sc.dma_sem("wb%d" % i_)
        t0 = {"dma": 0, "cast": 0}

        def t0_dma(c):
            st = c % NSTG
            pcs = w_pieces(worder[c])
            ntot = sum(pc[3] for pc in pcs)
            sc.dma("sp", "stg%d" % st,
                   [lambda q, pc=pc: q.dma_start(out=STG[:, st, pc[4]:pc[4] + pc[3]],
                                                 in_=F32T[pc[0]][pc[1]:pc[1] + 128, pc[2]:pc[2] + pc[3]]) for pc in pcs],
                   writes=[("STG", st * 1024, st * 1024 + ntot)])

        def t0_cast(c):
            st = c % NSTG
            slot = c % nslot
            pcs = w_pieces(worder[c])
            ntot = sum(pc[3] for pc in pcs)
            sc.op("act", lambda a: a.activation(out=WR[:, slot, 0:ntot], in_=STG[:, st, 0:ntot], func=AF.Copy),
                  reads=[("STG", st * 1024, st * 1024 + ntot)], writes=[("WR", slot * 1024, slot * 1024 + ntot)])
            if len(tiles) > 1:
                sc.dma("sp", "wb%d" % slot,
                       [lambda q, pc=pc: q.dma_start(out=B16T[pc[0]][pc[1]:pc[1] + 128, pc[2]:pc[2] + pc[3]],
                                                     in_=WR[:, slot, pc[4]:pc[4] + pc[3]]) for pc in pcs],
                       reads=[("WR", slot * 1024, slot * 1024 + ntot)],
                       writes=[dreg(pc) for pc in pcs])

        def t0_advance(i):
            target = min(T0N, i + CA + 1)
            while t0["cast"] < target:
                while t0["dma"] <= t0["cast"]:
                    t0_dma(t0["dma"])
                    t0["dma"] += 1
                t0_cast(t0["cast"])
                t0["cast"] += 1
                while t0["dma"] < min(T0N, t0["cast"] + NSTG):
                    t0_dma(t0["dma"])
                    t0["dma"] += 1

        wstate["issued"] = T0N

        def wget(key, q=0):
            i = wstate["pos"]
            assert worder[i] == key, (worder[i], key)
            wstate["pos"] += 1
            if i < T0N:
                t0_advance(i)
            while wstate["issued"] < min(len(worder), i - q + nslot):
                issue_load(wstate["issued"])
                wstate["issued"] += 1
            return i % nslot

        def wap(slot, kk):
            return WR[:, slot, kk * 128:(kk + 1) * 128]

        def rW(slot, kk):
            return ("WR", slot * 1024 + kk * 128, slot * 1024 + (kk + 1) * 128)

        def load_x(ti):
            s0, w = tiles[ti]
            xi = ti % 2
            sc.dma("sp", "x%d" % xi,
                   lambda q: q.dma_start(out=X[xi][:, :, 0:w], in_=xT[:, :, s0:s0 + w].rearrange("k p s -> p k s")),
                   writes=[rX(xi, k, 0, w) for k in range(KD)])

        def rms_stats(xi, w):
            Xt = X[xi]
            b = new_bank()
            mms = []
            sc.op("act", lambda a: a.activation(out=DUM[:, 1:2], in_=DUM[:, 0:1], func=AF.Ln),
                  reads=[("DUM", 0, 1)], writes=[("DUM", 1, 2)])
            for k in range(KD):
                sc.op("act", lambda a, k=k: a.activation(out=SQ[:, k, 0:w], in_=Xt[:, k, 0:w], func=AF.Square),
                      reads=[rX(xi, k, 0, w)], writes=[r3("SQ", WM, k, 0, w)])
            for k in range(KD):
                mms.append((lambda pe, k=k: pe.matmul(PS[b][:, 0:w], lhsT=ONES[:], rhs=SQ[:, k, 0:w],
                                                      start=(k == 0), stop=(k == KD - 1)),
                            [("ONES", 0, 128), r3("SQ", WM, k, 0, w)]))
                if k < KD - 1:
                    mms.append((lambda pe, k=k: pe.matmul(PS[b][:, 0:w], lhsT=ZEROS[:], rhs=SQ[:, k, 0:w],
                                                          start=False, stop=False),
                                [("ZEROS", 0, 128), r3("SQ", WM, k, 0, w)]))
            sc.mm_group(rP(b), mms)
            sc.op("act", lambda a: a.activation(out=PS[b][:, 0:w], in_=PS[b][:, 0:w], func=AF.Ln,
                                                scale=1.0 / D, bias=EPSC[:, 0:1]),
                  reads=[rP(b), ("EPSC", 0, 1)], writes=[rP(b)])
            sc.op("act", lambda a: a.activation(out=PS[b][:, 0:w], in_=PS[b][:, 0:w], func=AF.Exp, scale=-0.5),
                  reads=[rP(b)], writes=[rP(b)])
            return b

        def norm_to_h(xi, w, gcol, car):
            Xt = X[xi]
            rb = rms_stats(xi, w)
            sc.op("pool", lambda g: g.tensor_copy(out=H[:, :, 0:HALO], in_=HCAR[:, car]),
                  reads=[("HCAR", car * KD * HALO, (car + 1) * KD * HALO)],
                  writes=[rH(k, 0, HALO) for k in range(KD)])
            for k in range(KD):
                sc.op("dve", lambda v, k=k: v.scalar_tensor_tensor(
                    out=H[:, k, HALO:HALO + w], in0=Xt[:, k, 0:w], scalar=V[:, gcol + k:gcol + k + 1],
                    in1=PS[rb][:, 0:w], op0=ALU.mult, op1=ALU.mult),
                    reads=[rX(xi, k, 0, w), ("V", 0, NV), rP(rb)], writes=[rH(k, HALO, HALO + w)])
            sc.op("pool", lambda g: g.tensor_copy(out=HCAR[:, car], in_=H[:, :, w:w + HALO]),
                  reads=[rH(k, w, w + HALO) for k in range(KD)],
                  writes=[("HCAR", car * KD * HALO, (car + 1) * KD * HALO)])

        def proj_group(key, b, c0, n):
            slot = wget(key)
            mms = []
            for k in range(KD):
                mms.append((lambda pe, k=k: pe.matmul(PS[b][:, 0:n], lhsT=wap(slot, k), rhs=H[:, k, c0:c0 + n],
                                                      start=(k == 0), stop=(k == KD - 1)),
                            [rW(slot, k), rH(k, c0, c0 + n)]))
            sc.mm_group(rP(b), mms)

        def proj_groups_kmajor(keys, banks, c0, n):
            slots = [wget(key, q=qi) for qi, key in enumerate(keys)]
            seq = []
            for k in range(KD):
                for gi in range(len(keys)):
                    seq.append((lambda pe, k=k, gi=gi: pe.matmul(
                        PS[banks[gi]][:, 0:n], lhsT=wap(slots[gi], k), rhs=H[:, k, c0:c0 + n],
                        start=(k == 0), stop=(k == KD - 1)),
                        [rW(slots[gi], k), rH(k, c0, c0 + n)], gi, k == KD - 1))
            sc.mm_interleaved([rP(b) for b in banks], seq)

        cur = {"ti": 0, "l": 0}

        def dbg_on():
            return debug and cur["ti"] == debug_tile and cur["l"] == 0

        def mixer(l, xi, w):
            vb = l * NVL
            db = l * NDV
            n3 = w + HALO
            vcol = lambda c: V[:, vb + c:vb + c + 1]
            dcol = lambda c: DV[:, db + c:db + c + 1]
            rV = ("V", 0, NV)
            rDV = ("DV", db, db + NDV)

            stA = {}

            NPRE = 3
            pre_banks = [new_bank() for _ in range(NPRE)]
            proj_groups_kmajor([("in", l, j, 0) for j in range(NPRE)], pre_banks, 0, n3)

            def lru_a_front(j):
                if j < NPRE:
                    b = pre_banks[j]
                else:
                    b = new_bank()
                    proj_group(("in", l, j, 0), b, 0, n3)
                t = nxt("T", 4)
                sc.op("act", lambda a: a.activation(out=T[:, t, 0:w], in_=PS[b][:, 3:3 + w], func=AF.Identity,
                                                    scale=vcol(V_CW + 3 * 8 + j), bias=vcol(V_CB + j)),
                      reads=[rP(b), rV], writes=[r3("T", WM, t, 0, w)])
                for tap in (2, 1, 0):
                    sc.op("dve", lambda v, tap=tap: v.scalar_tensor_tensor(
                        out=T[:, t, 0:w], in0=PS[b][:, tap:tap + w], scalar=vcol(V_CW + tap * 8 + j),
                        in1=T[:, t, 0:w], op0=ALU.mult, op1=ALU.add),
                        reads=[rP(b), rV, r3("T", WM, t, 0, w)], writes=[r3("T", WM, t, 0, w)])
                stA[j] = t

            def lru_a_cast(j):
                t = stA[j]
                sc.op("act", lambda a: a.activation(out=XCB[:, t, 0:w], in_=T[:, t, 0:w], func=AF.Copy),
                      reads=[r3("T", WM, t, 0, w)], writes=[r3("XCB", WM, t, 0, w)])

            stB = {}

            def lru_a_back_pe(j):
                t = stA[j]
                ba_, bi_ = new_bank(), new_bank()
                stB[j] = (ba_, bi_)
                for (bb, which) in ((ba_, 0), (bi_, 1)):
                    sc.mm_group(rP(bb), [(lambda pe, bb=bb, which=which: pe.matmul(
                        PS[bb][:, 0:w], lhsT=BD[:, l * 16 + 2 * j + which, :], rhs=XCB[:, t, 0:w], start=True, stop=True),
                        [("BD", (l * 16 + 2 * j + which) * 128, (l * 16 + 2 * j + which + 1) * 128),
                         r3("XCB", WM, t, 0, w)])])

            def lru_a_back(j):
                t = stA.pop(j)
                ba_, bi_ = stB.pop(j)
                h = nxt("TH")
                sc.op("act", lambda a: a.activation(out=TH[:, h, 0:w], in_=PS[ba_][:, 0:w], func=AF.Tanh,
                                                    scale=0.5, bias=dcol(DV_HBA + j)),
                      reads=[rP(ba_), rDV], writes=[r3("TH", WM, h, 0, w)])
                sc.op("act", lambda a: a.activation(out=A[:, j, 0:w], in_=TH[:, h, 0:w], func=AF.Exp,
                                                    scale=dcol(DV_HC + j), bias=dcol(DV_HC + j)),
                      reads=[r3("TH", WM, h, 0, w), rDV], writes=[r3("A", WM, j, 0, w)])
                sc.op("pool", lambda g: g.tensor_tensor(out=M[:, j, 0:w], in0=A[:, j, 0:w], in1=A[:, j, 0:w], op=ALU.mult),
                      reads=[r3("A", WM, j, 0, w)], writes=[r3("M", WM, j, 0, w)])
                sc.op("pool", lambda g: g.tensor_scalar(out=M[:, j, 0:w], in0=M[:, j, 0:w], scalar1=-1.0, scalar2=1.0,
                                                        op0=ALU.mult, op1=ALU.add),
                      reads=[r3("M", WM, j, 0, w)], writes=[r3("M", WM, j, 0, w)])
                sc.op("pool", lambda g: g.tensor_scalar(out=M[:, j, 0:w], in0=M[:, j, 0:w], scalar1=2.0, scalar2=0.0,
                                                        op0=ALU.min, op1=ALU.max),
                      reads=[r3("M", WM, j, 0, w)], writes=[r3("M", WM, j, 0, w)])
                sc.op("act", lambda a: a.activation(out=PS[bi_][:, 0:w], in_=PS[bi_][:, 0:w], func=AF.Tanh,
                                                    scale=0.5, bias=dcol(DV_HBX + j)),
                      reads=[rP(bi_), rDV], writes=[rP(bi_)])
                sc.op("dve", lambda v: v.scalar_tensor_tensor(
                    out=Q[:, j, 0:w], in0=PS[bi_][:, 0:w], scalar=1.0, in1=T[:, t, 0:w], op0=ALU.add, op1=ALU.mult),
                    reads=[rP(bi_), r3("T", WM, t, 0, w)], writes=[r3("Q", WM, j, 0, w)])

            def sc_chunk(j):
                bb, bc, bx = new_bank(), new_bank(), new_bank()
                proj_group(("in", l, 16 + j, 0), bb, 0, n3)
                proj_group(("in", l, 20 + j, 0), bc, 0, n3)
                proj_group(("in", l, 24 + j, 0), bx, 0, n3)
                c = nxt("CS")
                sc.op("act", lambda a: a.activation(out=CS[:, c, 0:w + 2], in_=PS[bc][:, 1:w + 3], func=AF.Copy),
                      reads=[rP(bc)], writes=[r3("CS", WM + 2, c, 0, w + 2)])
                sc.op("dve", lambda v: v.tensor_tensor(out=PS[bx][:, 1:w + 3], in0=PS[bx][:, 1:w + 3],
                                                       in1=CS[:, c, 0:w + 2], op=ALU.mult),
                      reads=[rP(bx), r3("CS", WM + 2, c, 0, w + 2)], writes=[rP(bx)])
                sc.op("act", lambda a: a.activation(out=CT[:, c, 0:w], in_=PS[bx][:, 3:3 + w], func=AF.Identity,
                                                    scale=vcol(V_SCW + 2 * 4 + j)),
                      reads=[rP(bx), rV], writes=[r3("CT", WM, c, 0, w)])
                for tap in (1, 0):
                    sc.op("dve", lambda v, tap=tap: v.scalar_tensor_tensor(
                        out=CT[:, c, 0:w], in0=PS[bx][:, 1 + tap:1 + tap + w], scalar=vcol(V_SCW + tap * 4 + j),
                        in1=CT[:, c, 0:w], op0=ALU.mult, op1=ALU.add),
                        reads=[rP(bx), rV, r3("CT", WM, c, 0, w)],
                        writes=[r3("CT", WM, c, 0, w)])
                sc.op("dve", lambda v: v.tensor_tensor(out=Y[:, 8 + j, 0:w], in0=PS[bb][:, 3:3 + w],
                                                       in1=CT[:, c, 0:w], op=ALU.mult),
                      reads=[rP(bb), r3("CT", WM, c, 0, w)], writes=[r3("Y", WM, 8 + j, 0, w)])


            for j in range(8 + SK):
                if j < 8:
                    lru_a_front(j)
                if 0 <= j - 1 < 8:
                    lru_a_cast(j - 1)
                if 0 <= j - SK < 8:
                    lru_a_back_pe(j - SK)
                    lru_a_back(j - SK)
                if j in SC_AT:
                    sc_chunk(SC_AT[j])

            rM = [r3("M", WM, k, 0, w) for k in range(KD)]
            for pc in range(4):
                sc.op("act", lambda a, pc=pc: a.activation(out=M[:, 2 * pc:2 * pc + 2, 0:w], in_=M[:, 2 * pc:2 * pc + 2, 0:w],
                                                           func=AF.Sqrt, scale=1.0, bias=1e-30),
                      reads=rM[2 * pc:2 * pc + 2], writes=rM[2 * pc:2 * pc + 2])
            for j in range(8):
                sc.op("pool", lambda g, j=j: g.tensor_tensor(out=Q[:, j, 0:w], in0=Q[:, j, 0:w], in1=M[:, j, 0:w], op=ALU.mult),
                      reads=[r3("Q", WM, j, 0, w), r3("M", WM, j, 0, w)], writes=[r3("Q", WM, j, 0, w)])
            if dbg_on():
                dump("A0", A[:, 0, 0:w], r3("A", WM, 0, 0, w), w)
                dump("M0", M[:, 0, 0:w], r3("M", WM, 0, 0, w), w)
                dump("Q0", Q[:, 0, 0:w], r3("Q", WM, 0, 0, w), w)
                dump("Y8", Y[:, 8, 0:w], r3("Y", WM, 8, 0, w), w, bf=True)

            for j in range(8):
                b = new_bank()
                proj_group(("in", l, 8 + j, 0), b, HALO, w)
                sc.op("act", lambda a: a.activation(out=PS[b][:, 0:w], in_=PS[b][:, 0:w], func=AF.Gelu_apprx_tanh),
                      reads=[rP(b)], writes=[rP(b)])
                hs = nxt("HS")
                sidx = l * KD + j
                sc.op("dve", lambda v: v.tensor_tensor_scan(
                    out=HS[:, hs, 0:w], data0=A[:, j, 0:w], data1=Q[:, j, 0:w], initial=HST[:, sidx:sidx + 1],
                    op0=ALU.mult, op1=ALU.add),
                    reads=[r3("A", WM, j, 0, w), r3("Q", WM, j, 0, w), ("HST", sidx, sidx + 1)],
                    writes=[r3("HS", WM, hs, 0, w)])
                sc.op("dve", lambda v: v.tensor_copy(out=HST[:, sidx:sidx + 1], in_=HS[:, hs, w - 1:w]),
                      reads=[r3("HS", WM, hs, w - 1, w)], writes=[("HST", sidx, sidx + 1)])
                sc.op("dve", lambda v: v.scalar_tensor_tensor(
                    out=Y[:, j, 0:w], in0=HS[:, hs, 0:w], scalar=0.5, in1=PS[b][:, 0:w], op0=ALU.mult, op1=ALU.mult),
                    reads=[r3("HS", WM, hs, 0, w), rP(b)], writes=[r3("Y", WM, j, 0, w)])

        def residual_proj(kind, l, xi, w, src_name, SRC, nk):
            Xt = X[xi]
            nparts = (nk + 7) // 8
            for m in range(8):
                b = new_bank()
                mms = []
                for part in range(nparts):
                    slot = wget((kind, l, m, part), q=part)
                    if dbg_on() and m == 0:
                        dump("W%s%d" % (kind, part), WR[:, slot, 0:512], ("WR", slot * 1024, slot * 1024 + 512), 512, bf=True)
                    for kk in range(min(8, nk - part * 8)):
                        k = part * 8 + kk
                        mms.append((lambda pe, slot=slot, kk=kk, k=k: pe.matmul(
                            PS[b][:, 0:w], lhsT=wap(slot, kk), rhs=SRC[:, k, 0:w], start=(k == 0), stop=(k == nk - 1)),
                            [rW(slot, kk), r3(src_name, WM, k, 0, w)]))
                sc.mm_group(rP(b), mms)
                sc.op("dve", lambda v, m=m: v.tensor_tensor(out=Xt[:, m, 0:w], in0=PS[b][:, 0:w], in1=Xt[:, m, 0:w], op=ALU.add),
                      reads=[rP(b), rX(xi, m, 0, w)], writes=[rX(xi, m, 0, w)])

        def out_proj(l, xi, w):
            Xt = X[xi]
            korder = [8, 9, 10, 11, 0, 1, 2, 3, 4, 5, 6, 7]
            for half in range(2):
                ms = [half * 4 + i for i in range(4)]
                banks = [new_bank() for _ in ms]
                slots = {}
                q = 0
                for m in ms:
                    for part in range(2):
                        slots[(m, part)] = wget(("out", l, m, part), q=q)
                        q += 1
                seq = []
                if half == 0:
                    order = [(ki, k, i) for i in range(len(ms)) for ki, k in enumerate(korder[:4])]
                    order += [(ki + 4, k, i) for ki, k in enumerate(korder[4:]) for i in range(len(ms))]
                else:
                    order = [(ki, k, i) for i in range(len(ms)) for ki, k in enumerate(korder)]
                for ki, k, i in order:
                    m = ms[i]
                    slot = slots[(m, k // 8)]
                    kk = k % 8
                    seq.append((lambda pe, b=banks[i], slot=slot, kk=kk, k=k, ki=ki: pe.matmul(
                        PS[b][:, 0:w], lhsT=wap(slot, kk), rhs=Y[:, k, 0:w],
                        start=(ki == 0), stop=(ki == len(korder) - 1)),
                        [rW(slot, kk), r3("Y", WM, k, 0, w)], i, ki == len(korder) - 1))
                sc.mm_interleaved([rP(b) for b in banks], seq)
                for i, m in enumerate(ms):
                    sc.op("dve", lambda v, m=m, b=banks[i]: v.tensor_tensor(
                        out=Xt[:, m, 0:w], in0=PS[b][:, 0:w], in1=Xt[:, m, 0:w], op=ALU.add),
                        reads=[rP(banks[i]), rX(xi, m, 0, w)], writes=[rX(xi, m, 0, w)])

        def ffn(l, xi, w, hook=None):
            vb = l * NVL
            vcol = lambda c: V[:, vb + c:vb + c + 1]
            rV = ("V", 0, NV)
            n2 = w + 2
            pre = [new_bank() for _ in range(4)]
            proj_groups_kmajor([("up", l, 0, 0), ("up", l, N_FF, 0), ("up", l, 1, 0), ("up", l, N_FF + 1, 0)],
                               pre, 1, n2)
            for j in range(N_FF):
                if j < 2:
                    bg, bu = pre[2 * j], pre[2 * j + 1]
                else:
                    bg, bu = new_bank(), new_bank()
                    proj_group(("up", l, j, 0), bg, 1, n2)
                    proj_group(("up", l, N_FF + j, 0), bu, 1, n2)
                r = nxt("G")
                for (bb, BUF, name, ch) in ((bg, G, "G", j), (bu, U, "U", N_FF + j)):
                    sc.op("act", lambda a, bb=bb, BUF=BUF, ch=ch: a.activation(
                        out=BUF[:, r, 0:w], in_=PS[bb][:, 2:2 + w], func=AF.Identity, scale=vcol(V_FCW + 2 * N_UP + ch)),
                        reads=[rP(bb), rV], writes=[r3(name, WM, r, 0, w)])
                    for tap in (1, 0):
                        sc.op("dve", lambda v, bb=bb, BUF=BUF, ch=ch, tap=tap: v.scalar_tensor_tensor(
                            out=BUF[:, r, 0:w], in0=PS[bb][:, tap:tap + w], scalar=vcol(V_FCW + tap * N_UP + ch),
                            in1=BUF[:, r, 0:w], op0=ALU.mult, op1=ALU.add),
                            reads=[rP(bb), rV, r3(name, WM, r, 0, w)], writes=[r3(name, WM, r, 0, w)])
                sc.op("act", lambda a: a.activation(out=GE[:, r, 0:w], in_=G[:, r, 0:w], func=AF.Gelu_apprx_tanh),
                      reads=[r3("G", WM, r, 0, w)], writes=[r3("GE", WM, r, 0, w)])
                sc.op("pool", lambda g: g.tensor_tensor(out=ACTB[:, j, 0:w], in0=GE[:, r, 0:w], in1=U[:, r, 0:w], op=ALU.mult),
                      reads=[r3("GE", WM, r, 0, w), r3("U", WM, r, 0, w)], writes=[r3("ACTB", WM, j, 0, w)])
                if hook is not None and j == 3:
                    hook()

        def final_norm_store(ti):
            s0, w = tiles[ti]
            xi = ti % 2
            Xt = X[xi]
            gcol = depth * NVL
            rb = rms_stats(xi, w)
            for k in range(KD):
                sc.op("dve", lambda v, k=k: v.scalar_tensor_tensor(
                    out=Xt[:, k, 0:w], in0=Xt[:, k, 0:w], scalar=V[:, gcol + k:gcol + k + 1],
                    in1=PS[rb][:, 0:w], op0=ALU.mult, op1=ALU.mult),
                    reads=[rX(xi, k, 0, w), ("V", 0, NV), rP(rb)], writes=[rX(xi, k, 0, w)])
            sc.dma("sp", "o%d" % xi,
                   lambda q: q.dma_start(out=outT[:, :, s0:s0 + w].rearrange("k p s -> p k s"), in_=Xt[:, :, 0:w]),
                   reads=[rX(xi, k, 0, w) for k in range(KD)])

        for ti, (s0, w) in enumerate(tiles):
            xi = ti % 2
            for l in range(depth):
                cur["ti"], cur["l"] = ti, l
                if l == depth - 1 and ti + 1 < len(tiles):
                    load_x(ti + 1)
                norm_to_h(xi, w, l * NVL + V_G1, l * 2 + 0)
                if dbg_on():
                    dump("X_in", X[xi][:, 0, 0:w], rX(xi, 0, 0, w), w)
                    dump("H1", H[:, 0, 0:w + HALO], rH(0, 0, w + HALO), w + HALO, bf=True)
                mixer(l, xi, w)
                if dbg_on():
                    for jj in range(1, 8):
                        dump("Y%d" % jj, Y[:, jj, 0:w], r3("Y", WM, jj, 0, w), w, bf=True)
                    dump("Y0", Y[:, 0, 0:w], r3("Y", WM, 0, 0, w), w, bf=True)
                    dump("Y11", Y[:, 11, 0:w], r3("Y", WM, 11, 0, w), w, bf=True)
                out_proj(l, xi, w)
                if dbg_on():
                    dump("X_mid", X[xi][:, 0, 0:w], rX(xi, 0, 0, w), w)
                norm_to_h(xi, w, l * NVL + V_G2, l * 2 + 1)
                ffn(l, xi, w, hook=(lambda ti=ti: final_norm_store(ti - 1)) if (l == 0 and ti > 0) else None)
                if dbg_on():
                    dump("ACT0", ACTB[:, 0, 0:w], r3("ACTB", WM, 0, 0, w), w, bf=True)
                    dump("ACT23", ACTB[:, 23, 0:w], r3("ACTB", WM, 23, 0, w), w, bf=True)
                residual_proj("dn", l, xi, w, "ACTB", ACTB, N_FF)
                if dbg_on():
                    dump("X_out", X[xi][:, 0, 0:w], rX(xi, 0, 0, w), w)
        final_norm_store(len(tiles) - 1)
        assert wstate["pos"] == len(worder)
        for n in ("o0", "o1"):
            sc.wait_dma_total("sp", n)
        build_program.stats = dict(cnt=dict(sc.cnt), nwaits=sc.nwaits, banks=bank_ctr[0])
        build_program.dbg_names = dbg_names
    return nc


def prepare_weights(inputs, depth=DEPTH):
    f = lambda a: np.ascontiguousarray(np.asarray(a, dtype=np.float32))
    w_in = f(inputs["w_in"])[:depth]
    w_up = f(inputs["w_up"])[:depth]
    w_out = f(inputs["w_out"])[:depth]
    w_dn = f(inputs["w_down"])[:depth]
    L = depth
    win_t = w_in.reshape(L, KD, 128, N_IN, 128).transpose(0, 3, 2, 1, 4).reshape(L * N_IN * 128, 1024)
    wup_t = w_up.reshape(L, KD, 128, N_UP, 128).transpose(0, 3, 2, 1, 4).reshape(L * N_UP * 128, 1024)
    wout_t = w_out.reshape(L, N_MIX, 128, KD, 128).transpose(0, 3, 2, 1, 4).reshape(L * KD * 128, N_MIX * 128)
    wdn_t = w_dn.reshape(L, N_FF, 128, KD, 128).transpose(0, 3, 2, 1, 4).reshape(L, KD, 128, 2, 1536)
    wdn_t = wdn_t.transpose(0, 1, 3, 2, 4).reshape(L * KD * 2 * 128, 1536)
    wa = f(inputs["lru_wa"])[:depth]
    wx = f(inputs["lru_wx"])[:depth]
    bd = np.zeros((128, L, 16, 128), np.float32)
    for l in range(L):
        for c in range(8):
            for which, src in ((0, wa), (1, wx)):
                bd[0:64, l, 2 * c + which, 0:64] = src[l, 2 * c]
                bd[64:128, l, 2 * c + which, 64:128] = src[l, 2 * c + 1]
    bd = bd.reshape(128, L * 16 * 128)
    NV = L * NVL + 8
    vecs = np.zeros((128, NV), np.float32)
    col = lambda v: np.asarray(v, np.float32).reshape(-1, 128).T
    for l in range(L):
        b = l * NVL
        vecs[:, b + V_G1:b + V_G1 + 8] = col(inputs["norm1_g"][l])
        for tap in range(4):
            vecs[:, b + V_CW + tap * 8:b + V_CW + tap * 8 + 8] = col(inputs["lru_conv_w"][l][tap])
        vecs[:, b + V_CB:b + V_CB + 8] = col(inputs["lru_conv_b"][l])
        vecs[:, b + V_BA:b + V_BA + 8] = col(inputs["lru_ba"][l])
        vecs[:, b + V_BX:b + V_BX + 8] = col(inputs["lru_bx"][l])
        vecs[:, b + V_LAM:b + V_LAM + 8] = col(inputs["lru_lambda"][l])
        for tap in range(3):
            vecs[:, b + V_SCW + tap * 4:b + V_SCW + tap * 4 + 4] = col(inputs["sc_conv_w"][l][tap])
        vecs[:, b + V_G2:b + V_G2 + 8] = col(inputs["norm2_g"][l])
        for tap in range(3):
            vecs[:, b + V_FCW + tap * N_UP:b + V_FCW + (tap + 1) * N_UP] = col(inputs["ffn_conv_w"][l][tap])
    vecs[:, L * NVL:L * NVL + 8] = col(inputs["final_g"])
    return dict(win_t=np.ascontiguousarray(win_t), wup_t=np.ascontiguousarray(wup_t),
                wout_t=np.ascontiguousarray(wout_t), wdn_t=np.ascontiguousarray(wdn_t),
                bd_t=np.ascontiguousarray(bd), vecs=vecs)


_PROGRAM_CACHE = {}


def run(inputs, S=SEQ, depth=DEPTH, ncores=NCORES, W=WMAX, debug=False):
    x = np.asarray(inputs["x"], dtype=np.float32)
    assert x.shape[0] == ncores and x.shape[1] == S
    wts = prepare_weights(inputs, depth)
    key = (S, depth, W, debug)
    if key not in _PROGRAM_CACHE:
        _PROGRAM_CACHE[key] = build_program(S=S, depth=depth, W=W, debug=debug)
    nc = _PROGRAM_CACHE[key]
    in_maps = []
    for b in range(ncores):
        xT = np.ascontiguousarray(x[b].T).reshape(KD, 128, S)
        m = dict(wts)
        m["xT"] = xT
        in_maps.append(m)
    res = run_bass_kernel_spmd(nc, in_maps, core_ids=list(range(ncores)))
    out = np.empty((ncores, S, D), np.float32)
    for b in range(ncores):
        out[b] = np.asarray(res.results[b]["outT"], dtype=np.float32).reshape(D, S).T
    if debug:
        run.dbg = [{n: np.asarray(res.results[b]["dbg_b" if bf else "dbg_f"][i], dtype=np.float32)[:, :cnt]
                    for i, (n, bf, cnt) in enumerate(build_program.dbg_names)} for b in range(ncores)]
    return out


def kernel(**inputs):
    return run(inputs)
```

```python
import numpy as np
from contextlib import ExitStack

import concourse.bass as bass
import concourse.mybir as mybir
from concourse.bass_utils import run_bass_kernel_spmd

F32 = mybir.dt.float32
BF16 = mybir.dt.bfloat16
AF = mybir.ActivationFunctionType
ALU = mybir.AluOpType

D = 1024
KD = D // 128
DEPTH = 2
SEQ = 4096
NCORES = 8
D_LRU = 1024
D_SC = 512
D_IN = 3584
D_FF = 3072
EPS = 1e-6
N_IN = D_IN // 128
N_UP = 2 * D_FF // 128
N_FF = D_FF // 128
N_MIX = (D_LRU + D_SC) // 128
HALO = 3
WMAX = 456
NSLOT = 13
SAME_ENGINE_SYNC = True
FUSE_WAITS = True
SAME_ENGINE_MIN = 200

V_G1 = 0
V_CW = 8
V_CB = 40
V_BA = 48
V_BX = 56
V_LAM = 64
V_SCW = 72
V_G2 = 84
V_FCW = 92
NVL = 92 + 144
DV_HBA = 0
DV_HBX = 8
DV_C = 16
DV_HC = 24
NDV_BASE = 32
NDV = 32 + 40


class Sched:
    def __init__(self, nc, es):
        self.nc = nc
        self.es = es
        self.eng = {"pe": nc.tensor, "act": nc.scalar, "dve": nc.vector,
                    "pool": nc.gpsimd, "sp": nc.sync}
        self.compute = ("pe", "act", "dve", "pool")
        self.esem = {e: es.enter_context(nc.semaphore("sem_" + e)) for e in self.compute}
        self.cnt = {e: 0 for e in self.compute}
        self.dsem = {}
        self.waited = {e: {} for e in self.eng}
        self.recs = {}
        self.nwaits = 0

    def _sem(self, name):
        if name.startswith("E:"):
            return self.esem[name[2:]]
        return self.dsem[name][0]

    def dma_sem(self, name):
        if name not in self.dsem:
            self.dsem[name] = [self.es.enter_context(self.nc.semaphore("dsem_" + name)), 0]
        return name

    def _deps(self, e, reads, writes):
        deps = {}

        def add(rec):
            if rec[5] == e and e in self.compute and not SAME_ENGINE_SYNC and rec[1] - rec[0] >= SAME_ENGINE_MIN:
                return
            if deps.get(rec[3], 0) < rec[4]:
                deps[rec[3]] = rec[4]

        for (buf, lo, hi) in reads:
            for rec in self.recs.get(buf, ()):
                if rec[2] and rec[0] < hi and lo < rec[1]:
                    add(rec)
        for (buf, lo, hi) in writes:
            for rec in self.recs.get(buf, ()):
                if rec[0] < hi and lo < rec[1]:
                    add(rec)
        return deps

    def _wait(self, e, deps, keep_last=False):
        w = self.waited[e]
        need = [(name, val) for name, val in deps.items() if w.get(name, 0) < val]
        last = None
        if keep_last and need and FUSE_WAITS:
            last = need.pop()
        for name, val in need:
            self.eng[e].wait_ge(self._sem(name), val)
            w[name] = val
            self.nwaits += 1
        if last is not None:
            w[last[0]] = last[1]
        return last

    def _fuse(self, ins, last):
        if last is not None:
            ins._wait_ge(self._sem(last[0]), last[1])

    def _record(self, e, reads, writes, name, val):
        for (buf, lo, hi) in writes:
            lst = self.recs.setdefault(buf, [])
            lst[:] = [r for r in lst if not (lo <= r[0] and r[1] <= hi)]
            lst.append([lo, hi, True, name, val, e])
        for (buf, lo, hi) in reads:
            lst = self.recs.setdefault(buf, [])
            lst[:] = [r for r in lst if not ((not r[2]) and r[5] == e and lo <= r[0] and r[1] <= hi)]
            lst.append([lo, hi, False, name, val, e])

    def op(self, e, fn, reads=(), writes=()):
        last = self._wait(e, self._deps(e, reads, writes), keep_last=True)
        ins = fn(self.eng[e])
        self._fuse(ins, last)
        self.cnt[e] += 1
        ins.then_inc(self.esem[e], 1)
        self._record(e, reads, writes, "E:" + e, self.cnt[e])

    def mm_group(self, out_region, mms):
        e = "pe"
        ticket = self.cnt[e] + 1
        bank_wait = self._wait(e, self._deps(e, (), (out_region,)), keep_last=True)
        last = None
        for fn, reads in mms:
            self._wait(e, self._deps(e, reads, ()))
            last = fn(self.eng[e])
            if bank_wait is not None:
                self._fuse(last, bank_wait)
                bank_wait = None
            self._record(e, reads, (), "E:pe", ticket)
        last.then_inc(self.esem[e], 1)
        self.cnt[e] = ticket
        self._record(e, (), (out_region,), "E:pe", ticket)

    def mm_interleaved(self, bank_regions, seq):
        e = "pe"
        opened = set()
        pending = []
        for fn, reads, bi, last in seq:
            if bi not in opened:
                opened.add(bi)
                self._wait(e, self._deps(e, (), (bank_regions[bi],)))
            self._wait(e, self._deps(e, reads, ()))
            ins = fn(self.eng[e])
            if last:
                self.cnt[e] += 1
                ins.then_inc(self.esem[e], 1)
                t = self.cnt[e]
                for r in pending:
                    self._record(e, r, (), "E:pe", t)
                pending = []
                self._record(e, reads, (), "E:pe", t)
                self._record(e, (), (bank_regions[bi],), "E:pe", t)
            else:
                pending.append(reads)
        assert not pending

    def dma(self, e, semname, fns, reads=(), writes=()):
        if not isinstance(fns, (list, tuple)):
            fns = [fns]
        self._wait(e, self._deps(e, reads, writes))
        h = self.dsem[semname]
        for fn in fns:
            fn(self.eng[e]).then_inc(h[0], 16)
            h[1] += 16
        self._record("dma:" + semname, reads, writes, semname, h[1])

    def wait_dma_total(self, e, semname):
        h = self.dsem[semname]
        if self.waited[e].get(semname, 0) < h[1]:
            self.eng[e].wait_ge(h[0], h[1])
            self.waited[e][semname] = h[1]


def build_program(S=SEQ, depth=DEPTH, W=WMAX, nslot=NSLOT, debug=False):
    nc = bass.Bass("TRN2", target_bir_lowering=False)
    dbg_names = []
    if debug:
        dbg_f = nc.dram_tensor("dbg_f", [64, 128, 512], F32, kind="ExternalOutput").ap()
        dbg_b = nc.dram_tensor("dbg_b", [64, 128, 512], BF16, kind="ExternalOutput").ap()
    tiles = [(s0, min(W, S - s0)) for s0 in range(0, S, W)]
    WM = W
    debug_tile = (debug - 1) if debug else -1
    NV = depth * NVL + 8

    xT = nc.dram_tensor("xT", [KD, 128, S], F32, kind="ExternalInput").ap()
    win_f = nc.dram_tensor("win_t", [depth * N_IN * 128, 1024], F32, kind="ExternalInput").ap()
    wup_f = nc.dram_tensor("wup_t", [depth * N_UP * 128, 1024], F32, kind="ExternalInput").ap()
    wout_f = nc.dram_tensor("wout_t", [depth * KD * 128, 1536], F32, kind="ExternalInput").ap()
    wdn_f = nc.dram_tensor("wdn_t", [depth * KD * 128 * 2, 1536], F32, kind="ExternalInput").ap()
    bd_f = nc.dram_tensor("bd_t", [128, depth * 16 * 128], F32, kind="ExternalInput").ap()
    vecs = nc.dram_tensor("vecs", [128, NV], F32, kind="ExternalInput").ap()
    outT = nc.dram_tensor("outT", [KD, 128, S], F32, kind="ExternalOutput").ap()
    win_b = nc.dram_tensor("win_b", [depth * N_IN * 128, 1024], BF16, kind="Internal").ap()
    wup_b = nc.dram_tensor("wup_b", [depth * N_UP * 128, 1024], BF16, kind="Internal").ap()
    wout_b = nc.dram_tensor("wout_b", [depth * KD * 128, 1536], BF16, kind="Internal").ap()
    wdn_b = nc.dram_tensor("wdn_b", [depth * KD * 128 * 2, 1536], BF16, kind="Internal").ap()

    with ExitStack() as es:
        E = es.enter_context
        sb = lambda name, shape, dt: E(nc.sbuf_tensor(name, shape, dt))
        X = [sb("X0", [128, KD, WM], F32), sb("X1", [128, KD, WM], F32)]
        H = sb("H", [128, KD, WM + HALO], BF16)
        HCAR = sb("HCAR", [128, depth * 2, KD, HALO], BF16)
        SQ = sb("SQ", [128, KD, WM], BF16)
        DUM = sb("DUM", [128, 2], F32)
        Y = sb("Y", [128, N_MIX, WM], BF16)
        Q = sb("Q", [128, KD, WM], F32)
        A = sb("A", [128, KD, WM], F32)
        M = sb("M", [128, KD, WM], F32)
        T = sb("T", [128, 4, WM], F32)
        XCB = sb("XCB", [128, 4, WM], BF16)
        TH = sb("TH", [128, 2, WM], F32)
        HS = sb("HS", [128, 2, WM], F32)
        CS = sb("CS", [128, 2, WM + 2], F32)
        CT = sb("CT", [128, 2, WM], F32)
        ACTB = sb("ACTB", [128, N_FF, WM], BF16)
        G = sb("G", [128, 2, WM], F32)
        U = sb("U", [128, 2, WM], F32)
        GE = sb("GE", [128, 2, WM], F32)
        V = sb("V", [128, NV], F32)
        DV = sb("DV", [128, depth * NDV], F32)
        HST = sb("HST", [128, depth * KD], F32)
        ONES = sb("ONES", [128, 128], BF16)
        BD = sb("BD", [128, depth * 16, 128], BF16)
        WR = sb("WR", [128, nslot, 1024], BF16)
        PS = [E(nc.psum_tensor("PS%d" % i, [128, 512], F32)) for i in range(8)]

        sc = Sched(nc, es)
        for i in range(nslot):
            sc.dma_sem("w%d" % i)
        for n in ("x0", "x1", "o0", "o1", "const"):
            sc.dma_sem(n)

        sc.dma_sem("dbg")

        def dump(name, ap, region, n, bf=False):
            if not debug:
                return
            idx = len(dbg_names)
            dbg_names.append((name, bf, n))
            dst = (dbg_b if bf else dbg_f)[idx, :, 0:n]
            sc.dma("sp", "dbg", lambda q: q.dma_start(out=dst, in_=ap), reads=[region])
            sc.wait_dma_total("sp", "dbg")

        def r3(name, L, k, a, b):
            return (name, k * L + a, k * L + b)

        def rX(xi, k, a, b):
            return r3("X%d" % xi, WM, k, a, b)

        def rH(k, a, b):
            return r3("H", WM + HALO, k, a, b)

        rot = {}

        def nxt(name, n=2):
            rot[name] = (rot.get(name, -1) + 1) % n
            return rot[name]

        bank_ctr = [0]

        def new_bank():
            b = bank_ctr[0] % 8
            bank_ctr[0] += 1
            return b

        def rP(b):
            return ("PS%d" % b, 0, 512)

        s0_, w_ = tiles[0]
        sc.dma("sp", "x0",
               lambda q: q.dma_start(out=X[0][:, :, 0:w_], in_=xT[:, :, s0_:s0_ + w_].rearrange("k p s -> p k s")),
               writes=[rX(0, k, 0, w_) for k in range(KD)])
        sc.dma("sp", "const", lambda q: q.dma_start(out=V[:], in_=vecs), writes=[("V", 0, NV)])
        for l in range(depth):
            sc.dma("pool", sc.dma_sem("bd%d" % l),
                   lambda q, l=l: q.dma_start(out=BD[:, l * 16:(l + 1) * 16, :],
                                              in_=bd_f[:, l * 2048:(l + 1) * 2048].rearrange("p (a b) -> p a b", b=128)),
                   writes=[("BD", l * 16 * 128, (l + 1) * 16 * 128)])
        EPSC = sb("EPSC", [128, 2], F32)
        sc.op("dve", lambda v: v.memset(EPSC[:], EPS), writes=[("EPSC", 0, 2)])
        sc.op("dve", lambda v: v.memset(DUM[:], 1.0), writes=[("DUM", 0, 2)])
        sc.op("dve", lambda v: v.memset(ONES[:], 1.0), writes=[("ONES", 0, 128)])
        sc.op("dve", lambda v: v.memset(HST[:], 0.0), writes=[("HST", 0, depth * KD)])
        sc.op("pool", lambda g: g.memset(HCAR[:], 0.0),
              writes=[("HCAR", 0, depth * 2 * KD * HALO)])
        for l in range(depth):
            vb = l * NVL
            db = l * NDV
            lam = V[:, vb + V_LAM:vb + V_LAM + 8]
            tcol = lambda i: DV[:, db + NDV_BASE + 8 * i:db + NDV_BASE + 8 * i + 8]
            rD = ("DV", db, db + NDV)
            rVv = ("V", 0, NV)
            dv = lambda fn, rd=(rD,): sc.op("dve", fn, reads=list(rd), writes=[rD])
            sc.op("act", lambda a: a.activation(out=tcol(0), in_=lam, func=AF.Abs), reads=[rVv], writes=[rD])
            sc.op("act", lambda a: a.activation(out=tcol(1), in_=tcol(0), func=AF.Exp, scale=-1.0),
                  reads=[rD], writes=[rD])
            dv(lambda v: v.tensor_scalar(out=tcol(2), in0=tcol(1), scalar1=2.0, scalar2=None, op0=ALU.add))
            dv(lambda v: v.reciprocal(out=tcol(2), in_=tcol(2)))
            dv(lambda v: v.tensor_tensor(out=tcol(2), in0=tcol(2), in1=tcol(1), op=ALU.mult))
            dv(lambda v: v.tensor_tensor(out=tcol(3), in0=tcol(2), in1=tcol(2), op=ALU.mult))
            dv(lambda v: v.tensor_scalar(out=tcol(4), in0=tcol(3), scalar1=1.0 / 11.0, scalar2=1.0 / 9.0,
                                         op0=ALU.mult, op1=ALU.add))
            for cst in (1.0 / 7.0, 1.0 / 5.0, 1.0 / 3.0, 1.0):
                dv(lambda v: v.tensor_tensor(out=tcol(4), in0=tcol(4), in1=tcol(3), op=ALU.mult))
                dv(lambda v, cst=cst: v.tensor_scalar(out=tcol(4), in0=tcol(4), scalar1=cst, scalar2=None, op0=ALU.add))
            dv(lambda v: v.tensor_tensor(out=tcol(4), in0=tcol(4), in1=tcol(2), op=ALU.mult))
            dv(lambda v: v.tensor_scalar(out=tcol(0), in0=lam, scalar1=-1.0, scalar2=0.0, op0=ALU.mult, op1=ALU.max),
               rd=(rVv, rD))
            dv(lambda v: v.scalar_tensor_tensor(out=tcol(0), in0=tcol(4), scalar=2.0, in1=tcol(0),
                                                op0=ALU.mult, op1=ALU.add))
            dv(lambda v: v.tensor_scalar(out=DV[:, db + DV_C:db + DV_C + 8], in0=tcol(0), scalar1=-8.0, scalar2=None,
                                         op0=ALU.mult))
            dv(lambda v: v.tensor_scalar(out=DV[:, db + DV_HC:db + DV_HC + 8], in0=tcol(0), scalar1=-4.0, scalar2=None,
                                         op0=ALU.mult))
            sc.op("dve", lambda v, vb=vb, db=db: v.tensor_scalar(
                out=DV[:, db + DV_HBA:db + DV_HBA + 8], in0=V[:, vb + V_BA:vb + V_BA + 8],
                scalar1=0.5, scalar2=None, op0=ALU.mult),
                reads=[("V", 0, NV)], writes=[("DV", db + DV_HBA, db + DV_HBA + 8)])
            sc.op("dve", lambda v, vb=vb, db=db: v.tensor_scalar(
                out=DV[:, db + DV_HBX:db + DV_HBX + 8], in0=V[:, vb + V_BX:vb + V_BX + 8],
                scalar1=0.5, scalar2=None, op0=ALU.mult),
                reads=[("V", 0, NV)], writes=[("DV", db + DV_HBX, db + DV_HBX + 8)])

        dump("DV", DV[:, 0:NDV], ("DV", 0, NDV), NDV)

        def weight_order():
            for (s0, w) in tiles:
                for l in range(depth):
                    for j in range(8 + SK):
                        if j < 8:
                            yield ("in", l, j, 0)
                        if j in SC_AT:
                            for base in (16, 20, 24):
                                yield ("in", l, base + SC_AT[j], 0)
                    for j in range(8):
                        yield ("in", l, 8 + j, 0)
                    for m in range(8):
                        yield ("out", l, m, 0)
                        yield ("out", l, m, 1)
                    for j in range(N_FF):
                        yield ("up", l, j, 0)
                        yield ("up", l, N_FF + j, 0)
                    for m in range(8):
                        for part in range(3):
                            yield ("dn", l, m, part)

        SK = 3
        SC_AT = {2: 0, 4: 1, 6: 2, 8: 3}
        worder = list(weight_order())
        wstate = {"issued": 0, "pos": 0}

        F32T = {"win": win_f, "wup": wup_f, "wout": wout_f, "wdn": wdn_f}
        B16T = {"win": win_b, "wup": wup_b, "wout": wout_b, "wdn": wdn_b}

        def w_pieces(key):
            kind, l, idx, part = key
            if kind == "in":
                return [("win", (l * N_IN + idx) * 128, 0, 1024, 0)]
            if kind == "up":
                return [("wup", (l * N_UP + idx) * 128, 0, 1024, 0)]
            if kind == "out":
                r = (l * KD + idx) * 128
                return [("wout", r, 0, 1024, 0)] if part == 0 else [("wout", r, 1024, 512, 0)]
            r = (l * KD + idx) * 256
            if part == 0:
                return [("wdn", r, 0, 1024, 0)]
            if part == 1:
                return [("wdn", r, 1024, 512, 0), ("wdn", r + 128, 0, 512, 512)]
            return [("wdn", r + 128, 512, 1024, 0)]

        def dreg(pc):
            name, r, c0, n, d = pc
            return ("%s_b:%d" % (name, c0), r, r + 128)

        def issue_load(i):
            slot = i % nslot
            pcs = w_pieces(worder[i])
            ntot = sum(pc[3] for pc in pcs)
            sc.dma("sp", "w%d" % slot,
                   [lambda q, pc=pc: q.dma_start(out=WR[:, slot, pc[4]:pc[4] + pc[3]],
                                                 in_=B16T[pc[0]][pc[1]:pc[1] + 128, pc[2]:pc[2] + pc[3]]) for pc in pcs],
                   reads=[dreg(pc) for pc in pcs],
                   writes=[("WR", slot * 1024, slot * 1024 + ntot)])

        T0N = len(worder) // len(tiles)
        NSTG = 4
        CA = 2
        STG = sb("STG", [128, NSTG, 1024], F32)
        for i_ in range(NSTG):
            sc.dma_sem("stg%d" % i_)
        for i_ in range(nslot):
            sc.dma_sem("wb%d" % i_)
        t0 = {"dma": 0, "cast": 0}

        def t0_dma(c):
            st = c % NSTG
            pcs = w_pieces(worder[c])
            ntot = sum(pc[3] for pc in pcs)
            sc.dma("sp", "stg%d" % st,
                   [lambda q, pc=pc: q.dma_start(out=STG[:, st, pc[4]:pc[4] + pc[3]],
                                                 in_=F32T[pc[0]][pc[1]:pc[1] + 128, pc[2]:pc[2] + pc[3]]) for pc in pcs],
                   writes=[("STG", st * 1024, st * 1024 + ntot)])

        def t0_cast(c):
            st = c % NSTG
            slot = c % nslot
            pcs = w_pieces(worder[c])
            ntot = sum(pc[3] for pc in pcs)
            sc.op("act", lambda a: a.activation(out=WR[:, slot, 0:ntot], in_=STG[:, st, 0:ntot], func=AF.Copy),
                  reads=[("STG", st * 1024, st * 1024 + ntot)], writes=[("WR", slot * 1024, slot * 1024 + ntot)])
            if len(tiles) > 1:
                sc.dma("sp", "wb%d" % slot,
                       [lambda q, pc=pc: q.dma_start(out=B16T[pc[0]][pc[1]:pc[1] + 128, pc[2]:pc[2] + pc[3]],
                                                     in_=WR[:, slot, pc[4]:pc[4] + pc[3]]) for pc in pcs],
                       reads=[("WR", slot * 1024, slot * 1024 + ntot)],
                       writes=[dreg(pc) for pc in pcs])

        def t0_advance(i):
            target = min(T0N, i + CA + 1)
            while t0["cast"] < target:
                while t0["dma"] <= t0["cast"]:
                    t0_dma(t0["dma"])
                    t0["dma"] += 1
                t0_cast(t0["cast"])
                t0["cast"] += 1
                while t0["dma"] < min(T0N, t0["cast"] + NSTG):
                    t0_dma(t0["dma"])
                    t0["dma"] += 1

        wstate["issued"] = T0N

        def wget(key, q=0):
            i = wstate["pos"]
            assert worder[i] == key, (worder[i], key)
            wstate["pos"] += 1
            if i < T0N:
                t0_advance(i)
            while wstate["issued"] < min(len(worder), i - q + nslot):
                issue_load(wstate["issued"])
                wstate["issued"] += 1
            return i % nslot

        def wap(slot, kk):
            return WR[:, slot, kk * 128:(kk + 1) * 128]

        def rW(slot, kk):
            return ("WR", slot * 1024 + kk * 128, slot * 1024 + (kk + 1) * 128)

        def load_x(ti):
            s0, w = tiles[ti]
            xi = ti % 2
            sc.dma("sp", "x%d" % xi,
                   lambda q: q.dma_start(out=X[xi][:, :, 0:w], in_=xT[:, :, s0:s0 + w].rearrange("k p s -> p k s")),
                   writes=[rX(xi, k, 0, w) for k in range(KD)])

        def rms_stats(xi, w):
            Xt = X[xi]
            b = new_bank()
            mms = []
            sc.op("act", lambda a: a.activation(out=DUM[:, 1:2], in_=DUM[:, 0:1], func=AF.Ln),
                  reads=[("DUM", 0, 1)], writes=[("DUM", 1, 2)])
            for k in range(KD):
                sc.op("act", lambda a, k=k: a.activation(out=SQ[:, k, 0:w], in_=Xt[:, k, 0:w], func=AF.Square),
                      reads=[rX(xi, k, 0, w)], writes=[r3("SQ", WM, k, 0, w)])
            for k in range(KD):
                mms.append((lambda pe, k=k: pe.matmul(PS[b][:, 0:w], lhsT=ONES[:], rhs=SQ[:, k, 0:w],
                                                      start=(k == 0), stop=(k == KD - 1)),
                            [("ONES", 0, 128), r3("SQ", WM, k, 0, w)]))
            sc.mm_group(rP(b), mms)
            sc.op("act", lambda a: a.activation(out=PS[b][:, 0:w], in_=PS[b][:, 0:w], func=AF.Ln,
                                                scale=1.0 / D, bias=EPSC[:, 0:1]),
                  reads=[rP(b), ("EPSC", 0, 1)], writes=[rP(b)])
            sc.op("act", lambda a: a.activation(out=PS[b][:, 0:w], in_=PS[b][:, 0:w], func=AF.Exp, scale=-0.5),
                  reads=[rP(b)], writes=[rP(b)])
            return b

        def norm_to_h(xi, w, gcol, car):
            Xt = X[xi]
            rb = rms_stats(xi, w)
            sc.op("pool", lambda g: g.tensor_copy(out=H[:, :, 0:HALO], in_=HCAR[:, car]),
                  reads=[("HCAR", car * KD * HALO, (car + 1) * KD * HALO)],
                  writes=[rH(k, 0, HALO) for k in range(KD)])
            for k in range(KD):
                sc.op("dve", lambda v, k=k: v.scalar_tensor_tensor(
                    out=H[:, k, HALO:HALO + w], in0=Xt[:, k, 0:w], scalar=V[:, gcol + k:gcol + k + 1],
                    in1=PS[rb][:, 0:w], op0=ALU.mult, op1=ALU.mult),
                    reads=[rX(xi, k, 0, w), ("V", 0, NV), rP(rb)], writes=[rH(k, HALO, HALO + w)])
            sc.op("pool", lambda g: g.tensor_copy(out=HCAR[:, car], in_=H[:, :, w:w + HALO]),
                  reads=[rH(k, w, w + HALO) for k in range(KD)],
                  writes=[("HCAR", car * KD * HALO, (car + 1) * KD * HALO)])

        def proj_group(key, b, c0, n):
            slot = wget(key)
            mms = []
            for k in range(KD):
                mms.append((lambda pe, k=k: pe.matmul(PS[b][:, 0:n], lhsT=wap(slot, k), rhs=H[:, k, c0:c0 + n],
                                                      start=(k == 0), stop=(k == KD - 1)),
                            [rW(slot, k), rH(k, c0, c0 + n)]))
            sc.mm_group(rP(b), mms)

        def proj_groups_kmajor(keys, banks, c0, n):
            slots = [wget(key, q=qi) for qi, key in enumerate(keys)]
            seq = []
            for k in range(KD):
                for gi in range(len(keys)):
                    seq.append((lambda pe, k=k, gi=gi: pe.matmul(
                        PS[banks[gi]][:, 0:n], lhsT=wap(slots[gi], k), rhs=H[:, k, c0:c0 + n],
                        start=(k == 0), stop=(k == KD - 1)),
                        [rW(slots[gi], k), rH(k, c0, c0 + n)], gi, k == KD - 1))
            sc.mm_interleaved([rP(b) for b in banks], seq)

        cur = {"ti": 0, "l": 0}

        def dbg_on():
            return debug and cur["ti"] == debug_tile and cur["l"] == 0

        def mixer(l, xi, w):
            vb = l * NVL
            db = l * NDV
            n3 = w + HALO
            vcol = lambda c: V[:, vb + c:vb + c + 1]
            dcol = lambda c: DV[:, db + c:db + c + 1]
            rV = ("V", 0, NV)
            rDV = ("DV", db, db + NDV)

            stA = {}

            NPRE = 3
            pre_banks = [new_bank() for _ in range(NPRE)]
            proj_groups_kmajor([("in", l, j, 0) for j in range(NPRE)], pre_banks, 0, n3)

            def lru_a_front(j):
                if j < NPRE:
                    b = pre_banks[j]
                else:
                    b = new_bank()
                    proj_group(("in", l, j, 0), b, 0, n3)
                t = nxt("T", 4)
                sc.op("act", lambda a: a.activation(out=T[:, t, 0:w], in_=PS[b][:, 3:3 + w], func=AF.Identity,
                                                    scale=vcol(V_CW + 3 * 8 + j), bias=vcol(V_CB + j)),
                      reads=[rP(b), rV], writes=[r3("T", WM, t, 0, w)])
                for tap in (2, 1, 0):
                    sc.op("dve", lambda v, tap=tap: v.scalar_tensor_tensor(
                        out=T[:, t, 0:w], in0=PS[b][:, tap:tap + w], scalar=vcol(V_CW + tap * 8 + j),
                        in1=T[:, t, 0:w], op0=ALU.mult, op1=ALU.add),
                        reads=[rP(b), rV, r3("T", WM, t, 0, w)], writes=[r3("T", WM, t, 0, w)])
                stA[j] = t

            def lru_a_cast(j):
                t = stA[j]
                sc.op("act", lambda a: a.activation(out=XCB[:, t, 0:w], in_=T[:, t, 0:w], func=AF.Copy),
                      reads=[r3("T", WM, t, 0, w)], writes=[r3("XCB", WM, t, 0, w)])

            stB = {}

            def lru_a_back_pe(j):
                t = stA[j]
                ba_, bi_ = new_bank(), new_bank()
                stB[j] = (ba_, bi_)
                for (bb, which) in ((ba_, 0), (bi_, 1)):
                    sc.mm_group(rP(bb), [(lambda pe, bb=bb, which=which: pe.matmul(
                        PS[bb][:, 0:w], lhsT=BD[:, l * 16 + 2 * j + which, :], rhs=XCB[:, t, 0:w], start=True, stop=True),
                        [("BD", (l * 16 + 2 * j + which) * 128, (l * 16 + 2 * j + which + 1) * 128),
                         r3("XCB", WM, t, 0, w)])])

            def lru_a_back(j):
                t = stA.pop(j)
                ba_, bi_ = stB.pop(j)
                h = nxt("TH")
                sc.op("act", lambda a: a.activation(out=TH[:, h, 0:w], in_=PS[ba_][:, 0:w], func=AF.Tanh,
                                                    scale=0.5, bias=dcol(DV_HBA + j)),
                      reads=[rP(ba_), rDV], writes=[r3("TH", WM, h, 0, w)])
                sc.op("act", lambda a: a.activation(out=A[:, j, 0:w], in_=TH[:, h, 0:w], func=AF.Exp,
                                                    scale=dcol(DV_HC + j), bias=dcol(DV_HC + j)),
                      reads=[r3("TH", WM, h, 0, w), rDV], writes=[r3("A", WM, j, 0, w)])
                sc.op("pool", lambda g: g.tensor_tensor(out=M[:, j, 0:w], in0=A[:, j, 0:w], in1=A[:, j, 0:w], op=ALU.mult),
                      reads=[r3("A", WM, j, 0, w)], writes=[r3("M", WM, j, 0, w)])
                sc.op("pool", lambda g: g.tensor_scalar(out=M[:, j, 0:w], in0=M[:, j, 0:w], scalar1=-1.0, scalar2=1.0,
                                                        op0=ALU.mult, op1=ALU.add),
                      reads=[r3("M", WM, j, 0, w)], writes=[r3("M", WM, j, 0, w)])
                sc.op("pool", lambda g: g.tensor_scalar(out=M[:, j, 0:w], in0=M[:, j, 0:w], scalar1=2.0, scalar2=0.0,
                                                        op0=ALU.min, op1=ALU.max),
                      reads=[r3("M", WM, j, 0, w)], writes=[r3("M", WM, j, 0, w)])
                sc.op("act", lambda a: a.activation(out=PS[bi_][:, 0:w], in_=PS[bi_][:, 0:w], func=AF.Tanh,
                                                    scale=0.5, bias=dcol(DV_HBX + j)),
                      reads=[rP(bi_), rDV], writes=[rP(bi_)])
                sc.op("dve", lambda v: v.scalar_tensor_tensor(
                    out=Q[:, j, 0:w], in0=PS[bi_][:, 0:w], scalar=1.0, in1=T[:, t, 0:w], op0=ALU.add, op1=ALU.mult),
                    reads=[rP(bi_), r3("T", WM, t, 0, w)], writes=[r3("Q", WM, j, 0, w)])

            def sc_chunk(j):
                bb, bc, bx = new_bank(), new_bank(), new_bank()
                proj_group(("in", l, 16 + j, 0), bb, 0, n3)
                proj_group(("in", l, 20 + j, 0), bc, 0, n3)
                proj_group(("in", l, 24 + j, 0), bx, 0, n3)
                c = nxt("CS")
                sc.op("act", lambda a: a.activation(out=CS[:, c, 0:w + 2], in_=PS[bc][:, 1:w + 3], func=AF.Copy),
                      reads=[rP(bc)], writes=[r3("CS", WM + 2, c, 0, w + 2)])
                sc.op("dve", lambda v: v.tensor_tensor(out=PS[bx][:, 1:w + 3], in0=PS[bx][:, 1:w + 3],
                                                       in1=CS[:, c, 0:w + 2], op=ALU.mult),
                      reads=[rP(bx), r3("CS", WM + 2, c, 0, w + 2)], writes=[rP(bx)])
                sc.op("act", lambda a: a.activation(out=CT[:, c, 0:w], in_=PS[bx][:, 3:3 + w], func=AF.Identity,
                                                    scale=vcol(V_SCW + 2 * 4 + j)),
                      reads=[rP(bx), rV], writes=[r3("CT", WM, c, 0, w)])
                for tap in (1, 0):
                    sc.op("dve", lambda v, tap=tap: v.scalar_tensor_tensor(
                        out=CT[:, c, 0:w], in0=PS[bx][:, 1 + tap:1 + tap + w], scalar=vcol(V_SCW + tap * 4 + j),
                        in1=CT[:, c, 0:w], op0=ALU.mult, op1=ALU.add),
                        reads=[rP(bx), rV, r3("CT", WM, c, 0, w)],
                        writes=[r3("CT", WM, c, 0, w)])
                sc.op("dve", lambda v: v.tensor_tensor(out=Y[:, 8 + j, 0:w], in0=PS[bb][:, 3:3 + w],
                                                       in1=CT[:, c, 0:w], op=ALU.mult),
                      reads=[rP(bb), r3("CT", WM, c, 0, w)], writes=[r3("Y", WM, 8 + j, 0, w)])


            for j in range(8 + SK):
                if j < 8:
                    lru_a_front(j)
                if 0 <= j - 1 < 8:
                    lru_a_cast(j - 1)
                if 0 <= j - SK < 8:
                    lru_a_back_pe(j - SK)
                    lru_a_back(j - SK)
                if j in SC_AT:
                    sc_chunk(SC_AT[j])

            rM = [r3("M", WM, k, 0, w) for k in range(KD)]
            for pc in range(4):
                sc.op("act", lambda a, pc=pc: a.activation(out=M[:, 2 * pc:2 * pc + 2, 0:w], in_=M[:, 2 * pc:2 * pc + 2, 0:w],
                                                           func=AF.Sqrt, scale=1.0, bias=1e-30),
                      reads=rM[2 * pc:2 * pc + 2], writes=rM[2 * pc:2 * pc + 2])
            for j in range(8):
                sc.op("pool", lambda g, j=j: g.tensor_tensor(out=Q[:, j, 0:w], in0=Q[:, j, 0:w], in1=M[:, j, 0:w], op=ALU.mult),
                      reads=[r3("Q", WM, j, 0, w), r3("M", WM, j, 0, w)], writes=[r3("Q", WM, j, 0, w)])
            if dbg_on():
                dump("A0", A[:, 0, 0:w], r3("A", WM, 0, 0, w), w)
                dump("M0", M[:, 0, 0:w], r3("M", WM, 0, 0, w), w)
                dump("Q0", Q[:, 0, 0:w], r3("Q", WM, 0, 0, w), w)
                dump("Y8", Y[:, 8, 0:w], r3("Y", WM, 8, 0, w), w, bf=True)

            for j in range(8):
                b = new_bank()
                proj_group(("in", l, 8 + j, 0), b, HALO, w)
                sc.op("act", lambda a: a.activation(out=PS[b][:, 0:w], in_=PS[b][:, 0:w], func=AF.Gelu_apprx_tanh),
                      reads=[rP(b)], writes=[rP(b)])
                hs = nxt("HS")
                sidx = l * KD + j
                sc.op("dve", lambda v: v.tensor_tensor_scan(
                    out=HS[:, hs, 0:w], data0=A[:, j, 0:w], data1=Q[:, j, 0:w], initial=HST[:, sidx:sidx + 1],
                    op0=ALU.mult, op1=ALU.add),
                    reads=[r3("A", WM, j, 0, w), r3("Q", WM, j, 0, w), ("HST", sidx, sidx + 1)],
                    writes=[r3("HS", WM, hs, 0, w)])
                sc.op("dve", lambda v: v.tensor_copy(out=HST[:, sidx:sidx + 1], in_=HS[:, hs, w - 1:w]),
                      reads=[r3("HS", WM, hs, w - 1, w)], writes=[("HST", sidx, sidx + 1)])
                sc.op("dve", lambda v: v.scalar_tensor_tensor(
                    out=Y[:, j, 0:w], in0=HS[:, hs, 0:w], scalar=0.5, in1=PS[b][:, 0:w], op0=ALU.mult, op1=ALU.mult),
                    reads=[r3("HS", WM, hs, 0, w), rP(b)], writes=[r3("Y", WM, j, 0, w)])

        def residual_proj(kind, l, xi, w, src_name, SRC, nk):
            Xt = X[xi]
            nparts = (nk + 7) // 8
            for m in range(8):
                b = new_bank()
                mms = []
                for part in range(nparts):
                    slot = wget((kind, l, m, part), q=part)
                    if dbg_on() and m == 0:
                        dump("W%s%d" % (kind, part), WR[:, slot, 0:512], ("WR", slot * 1024, slot * 1024 + 512), 512, bf=True)
                    for kk in range(min(8, nk - part * 8)):
                        k = part * 8 + kk
                        mms.append((lambda pe, slot=slot, kk=kk, k=k: pe.matmul(
                            PS[b][:, 0:w], lhsT=wap(slot, kk), rhs=SRC[:, k, 0:w], start=(k == 0), stop=(k == nk - 1)),
                            [rW(slot, kk), r3(src_name, WM, k, 0, w)]))
                sc.mm_group(rP(b), mms)
                sc.op("dve", lambda v, m=m: v.tensor_tensor(out=Xt[:, m, 0:w], in0=PS[b][:, 0:w], in1=Xt[:, m, 0:w], op=ALU.add),
                      reads=[rP(b), rX(xi, m, 0, w)], writes=[rX(xi, m, 0, w)])

        def out_proj(l, xi, w):
            Xt = X[xi]
            korder = [8, 9, 10, 11, 0, 1, 2, 3, 4, 5, 6, 7]
            for half in range(2):
                ms = [half * 4 + i for i in range(4)]
                banks = [new_bank() for _ in ms]
                slots = {}
                q = 0
                for m in ms:
                    for part in range(2):
                        slots[(m, part)] = wget(("out", l, m, part), q=q)
                        q += 1
                seq = []
                if half == 0:
                    order = [(ki, k, i) for i in range(len(ms)) for ki, k in enumerate(korder[:4])]
                    order += [(ki + 4, k, i) for ki, k in enumerate(korder[4:]) for i in range(len(ms))]
                else:
                    order = [(ki, k, i) for i in range(len(ms)) for ki, k in enumerate(korder)]
                for ki, k, i in order:
                    m = ms[i]
                    slot = slots[(m, k // 8)]
                    kk = k % 8
                    seq.append((lambda pe, b=banks[i], slot=slot, kk=kk, k=k, ki=ki: pe.matmul(
                        PS[b][:, 0:w], lhsT=wap(slot, kk), rhs=Y[:, k, 0:w],
                        start=(ki == 0), stop=(ki == len(korder) - 1)),
                        [rW(slot, kk), r3("Y", WM, k, 0, w)], i, ki == len(korder) - 1))
                sc.mm_interleaved([rP(b) for b in banks], seq)
                for i, m in enumerate(ms):
                    sc.op("dve", lambda v, m=m, b=banks[i]: v.tensor_tensor(
                        out=Xt[:, m, 0:w], in0=PS[b][:, 0:w], in1=Xt[:, m, 0:w], op=ALU.add),
                        reads=[rP(banks[i]), rX(xi, m, 0, w)], writes=[rX(xi, m, 0, w)])

        def ffn(l, xi, w, hook=None):
            vb = l * NVL
            vcol = lambda c: V[:, vb + c:vb + c + 1]
            rV = ("V", 0, NV)
            n2 = w + 2
            pre = [new_bank() for _ in range(4)]
            proj_groups_kmajor([("up", l, 0, 0), ("up", l, N_FF, 0), ("up", l, 1, 0), ("up", l, N_FF + 1, 0)],
                               pre, 1, n2)
            for j in range(N_FF):
                if j < 2:
                    bg, bu = pre[2 * j], pre[2 * j + 1]
                else:
                    bg, bu = new_bank(), new_bank()
                    proj_group(("up", l, j, 0), bg, 1, n2)
                    proj_group(("up", l, N_FF + j, 0), bu, 1, n2)
                r = nxt("G")
                for (bb, BUF, name, ch) in ((bg, G, "G", j), (bu, U, "U", N_FF + j)):
                    sc.op("act", lambda a, bb=bb, BUF=BUF, ch=ch: a.activation(
                        out=BUF[:, r, 0:w], in_=PS[bb][:, 2:2 + w], func=AF.Identity, scale=vcol(V_FCW + 2 * N_UP + ch)),
                        reads=[rP(bb), rV], writes=[r3(name, WM, r, 0, w)])
                    for tap in (1, 0):
                        sc.op("dve", lambda v, bb=bb, BUF=BUF, ch=ch, tap=tap: v.scalar_tensor_tensor(
                            out=BUF[:, r, 0:w], in0=PS[bb][:, tap:tap + w], scalar=vcol(V_FCW + tap * N_UP + ch),
                            in1=BUF[:, r, 0:w], op0=ALU.mult, op1=ALU.add),
                            reads=[rP(bb), rV, r3(name, WM, r, 0, w)], writes=[r3(name, WM, r, 0, w)])
                sc.op("act", lambda a: a.activation(out=GE[:, r, 0:w], in_=G[:, r, 0:w], func=AF.Gelu_apprx_tanh),
                      reads=[r3("G", WM, r, 0, w)], writes=[r3("GE", WM, r, 0, w)])
                sc.op("pool", lambda g: g.tensor_tensor(out=ACTB[:, j, 0:w], in0=GE[:, r, 0:w], in1=U[:, r, 0:w], op=ALU.mult),
                      reads=[r3("GE", WM, r, 0, w), r3("U", WM, r, 0, w)], writes=[r3("ACTB", WM, j, 0, w)])
                if hook is not None and j == 3:
                    hook()

        def final_norm_store(ti):
            s0, w = tiles[ti]
            xi = ti % 2
            Xt = X[xi]
            gcol = depth * NVL
            rb = rms_stats(xi, w)
            for k in range(KD):
                sc.op("dve", lambda v, k=k: v.scalar_tensor_tensor(
                    out=Xt[:, k, 0:w], in0=Xt[:, k, 0:w], scalar=V[:, gcol + k:gcol + k + 1],
                    in1=PS[rb][:, 0:w], op0=ALU.mult, op1=ALU.mult),
                    reads=[rX(xi, k, 0, w), ("V", 0, NV), rP(rb)], writes=[rX(xi, k, 0, w)])
            sc.dma("sp", "o%d" % xi,
                   lambda q: q.dma_start(out=outT[:, :, s0:s0 + w].rearrange("k p s -> p k s"), in_=Xt[:, :, 0:w]),
                   reads=[rX(xi, k, 0, w) for k in range(KD)])

        for ti, (s0, w) in enumerate(tiles):
            xi = ti % 2
            for l in range(depth):
                cur["ti"], cur["l"] = ti, l
                if l == depth - 1 and ti + 1 < len(tiles):
                    load_x(ti + 1)
                norm_to_h(xi, w, l * NVL + V_G1, l * 2 + 0)
                if dbg_on():
                    dump("X_in", X[xi][:, 0, 0:w], rX(xi, 0, 0, w), w)
                    dump("H1", H[:, 0, 0:w + HALO], rH(0, 0, w + HALO), w + HALO, bf=True)
                mixer(l, xi, w)
                if dbg_on():
                    for jj in range(1, 8):
                        dump("Y%d" % jj, Y[:, jj, 0:w], r3("Y", WM, jj, 0, w), w, bf=True)
                    dump("Y0", Y[:, 0, 0:w], r3("Y", WM, 0, 0, w), w, bf=True)
                    dump("Y11", Y[:, 11, 0:w], r3("Y", WM, 11, 0, w), w, bf=True)
                out_proj(l, xi, w)
                if dbg_on():
                    dump("X_mid", X[xi][:, 0, 0:w], rX(xi, 0, 0, w), w)
                norm_to_h(xi, w, l * NVL + V_G2, l * 2 + 1)
                ffn(l, xi, w, hook=(lambda ti=ti: final_norm_store(ti - 1)) if (l == 0 and ti > 0) else None)
                if dbg_on():
                    dump("ACT0", ACTB[:, 0, 0:w], r3("ACTB", WM, 0, 0, w), w, bf=True)
                    dump("ACT23", ACTB[:, 23, 0:w], r3("ACTB", WM, 23, 0, w), w, bf=True)
                residual_proj("dn", l, xi, w, "ACTB", ACTB, N_FF)
                if dbg_on():
                    dump("X_out", X[xi][:, 0, 0:w], rX(xi, 0, 0, w), w)
        final_norm_store(len(tiles) - 1)
        assert wstate["pos"] == len(worder)
        for n in ("o0", "o1"):
            sc.wait_dma_total("sp", n)
        build_program.stats = dict(cnt=dict(sc.cnt), nwaits=sc.nwaits, banks=bank_ctr[0])
        build_program.dbg_names = dbg_names
    return nc


def prepare_weights(inputs, depth=DEPTH):
    f = lambda a: np.ascontiguousarray(np.asarray(a, dtype=np.float32))
    w_in = f(inputs["w_in"])[:depth]
    w_up = f(inputs["w_up"])[:depth]
    w_out = f(inputs["w_out"])[:depth]
    w_dn = f(inputs["w_down"])[:depth]
    L = depth
    win_t = w_in.reshape(L, KD, 128, N_IN, 128).transpose(0, 3, 2, 1, 4).reshape(L * N_IN * 128, 1024)
    wup_t = w_up.reshape(L, KD, 128, N_UP, 128).transpose(0, 3, 2, 1, 4).reshape(L * N_UP * 128, 1024)
    wout_t = w_out.reshape(L, N_MIX, 128, KD, 128).transpose(0, 3, 2, 1, 4).reshape(L * KD * 128, N_MIX * 128)
    wdn_t = w_dn.reshape(L, N_FF, 128, KD, 128).transpose(0, 3, 2, 1, 4).reshape(L, KD, 128, 2, 1536)
    wdn_t = wdn_t.transpose(0, 1, 3, 2, 4).reshape(L * KD * 2 * 128, 1536)
    wa = f(inputs["lru_wa"])[:depth]
    wx = f(inputs["lru_wx"])[:depth]
    bd = np.zeros((128, L, 16, 128), np.float32)
    for l in range(L):
        for c in range(8):
            for which, src in ((0, wa), (1, wx)):
                bd[0:64, l, 2 * c + which, 0:64] = src[l, 2 * c]
                bd[64:128, l, 2 * c + which, 64:128] = src[l, 2 * c + 1]
    bd = bd.reshape(128, L * 16 * 128)
    NV = L * NVL + 8
    vecs = np.zeros((128, NV), np.float32)
    col = lambda v: np.asarray(v, np.float32).reshape(-1, 128).T
    for l in range(L):
        b = l * NVL
        vecs[:, b + V_G1:b + V_G1 + 8] = col(inputs["norm1_g"][l])
        for tap in range(4):
            vecs[:, b + V_CW + tap * 8:b + V_CW + tap * 8 + 8] = col(inputs["lru_conv_w"][l][tap])
        vecs[:, b + V_CB:b + V_CB + 8] = col(inputs["lru_conv_b"][l])
        vecs[:, b + V_BA:b + V_BA + 8] = col(inputs["lru_ba"][l])
        vecs[:, b + V_BX:b + V_BX + 8] = col(inputs["lru_bx"][l])
        vecs[:, b + V_LAM:b + V_LAM + 8] = col(inputs["lru_lambda"][l])
        for tap in range(3):
            vecs[:, b + V_SCW + tap * 4:b + V_SCW + tap * 4 + 4] = col(inputs["sc_conv_w"][l][tap])
        vecs[:, b + V_G2:b + V_G2 + 8] = col(inputs["norm2_g"][l])
        for tap in range(3):
            vecs[:, b + V_FCW + tap * N_UP:b + V_FCW + (tap + 1) * N_UP] = col(inputs["ffn_conv_w"][l][tap])
    vecs[:, L * NVL:L * NVL + 8] = col(inputs["final_g"])
    return dict(win_t=np.ascontiguousarray(win_t), wup_t=np.ascontiguousarray(wup_t),
                wout_t=np.ascontiguousarray(wout_t), wdn_t=np.ascontiguousarray(wdn_t),
                bd_t=np.ascontiguousarray(bd), vecs=vecs)


_PROGRAM_CACHE = {}


def run(inputs, S=SEQ, depth=DEPTH, ncores=NCORES, W=WMAX, debug=False):
    x = np.asarray(inputs["x"], dtype=np.float32)
    assert x.shape[0] == ncores and x.shape[1] == S
    wts = prepare_weights(inputs, depth)
    key = (S, depth, W, debug)
    if key not in _PROGRAM_CACHE:
        _PROGRAM_CACHE[key] = build_program(S=S, depth=depth, W=W, debug=debug)
    nc = _PROGRAM_CACHE[key]
    in_maps = []
    for b in range(ncores):
        xT = np.ascontiguousarray(x[b].T).reshape(KD, 128, S)
        m = dict(wts)
        m["xT"] = xT
        in_maps.append(m)
    res = run_bass_kernel_spmd(nc, in_maps, core_ids=list(range(ncores)))
    out = np.empty((ncores, S, D), np.float32)
    for b in range(ncores):
        out[b] = np.asarray(res.results[b]["outT"], dtype=np.float32).reshape(D, S).T
    if debug:
        run.dbg = [{n: np.asarray(res.results[b]["dbg_b" if bf else "dbg_f"][i], dtype=np.float32)[:, :cnt]
                    for i, (n, bf, cnt) in enumerate(build_program.dbg_names)} for b in range(ncores)]
    return out


def kernel(**inputs):
    return run(inputs)
```

```python
import numpy as np
from contextlib import ExitStack

import concourse.bass as bass
import concourse.mybir as mybir
from concourse.bass_utils import run_bass_kernel_spmd

F32 = mybir.dt.float32
BF16 = mybir.dt.bfloat16
AF = mybir.ActivationFunctionType
ALU = mybir.AluOpType

D = 1024
KD = D // 128
DEPTH = 2
SEQ = 4096
NCORES = 8
D_LRU = 1024
D_SC = 512
D_IN = 3584
D_FF = 3072
EPS = 1e-6
N_IN = D_IN // 128
N_UP = 2 * D_FF // 128
N_FF = D_FF // 128
N_MIX = (D_LRU + D_SC) // 128
HALO = 3
WMAX = 456
NSLOT = 13
SAME_ENGINE_SYNC = True
FUSE_WAITS = True
SAME_ENGINE_MIN = 200

V_G1 = 0
V_CW = 8
V_CB = 40
V_BA = 48
V_BX = 56
V_LAM = 64
V_SCW = 72
V_G2 = 84
V_FCW = 92
NVL = 92 + 144
DV_HBA = 0
DV_HBX = 8
DV_C = 16
DV_HC = 24
NDV_BASE = 32
NDV = 32 + 40


class Sched:
    def __init__(self, nc, es):
        self.nc = nc
        self.es = es
        self.eng = {"pe": nc.tensor, "act": nc.scalar, "dve": nc.vector,
                    "pool": nc.gpsimd, "sp": nc.sync}
        self.compute = ("pe", "act", "dve", "pool")
        self.esem = {e: es.enter_context(nc.semaphore("sem_" + e)) for e in self.compute}
        self.cnt = {e: 0 for e in self.compute}
        self.dsem = {}
        self.waited = {e: {} for e in self.eng}
        self.recs = {}
        self.nwaits = 0

    def _sem(self, name):
        if name.startswith("E:"):
            return self.esem[name[2:]]
        return self.dsem[name][0]

    def dma_sem(self, name):
        if name not in self.dsem:
            self.dsem[name] = [self.es.enter_context(self.nc.semaphore("dsem_" + name)), 0]
        return name

    def _deps(self, e, reads, writes):
        deps = {}

        def add(rec):
            if rec[5] == e and e in self.compute and not SAME_ENGINE_SYNC and rec[1] - rec[0] >= SAME_ENGINE_MIN:
                return
            if deps.get(rec[3], 0) < rec[4]:
                deps[rec[3]] = rec[4]

        for (buf, lo, hi) in reads:
            for rec in self.recs.get(buf, ()):
                if rec[2] and rec[0] < hi and lo < rec[1]:
                    add(rec)
        for (buf, lo, hi) in writes:
            for rec in self.recs.get(buf, ()):
                if rec[0] < hi and lo < rec[1]:
                    add(rec)
        return deps

    def _wait(self, e, deps, keep_last=False):
        w = self.waited[e]
        need = [(name, val) for name, val in deps.items() if w.get(name, 0) < val]
        last = None
        if keep_last and need and FUSE_WAITS:
            last = need.pop()
        for name, val in need:
            self.eng[e].wait_ge(self._sem(name), val)
            w[name] = val
            self.nwaits += 1
        if last is not None:
            w[last[0]] = last[1]
        return last

    def _fuse(self, ins, last):
        if last is not None:
            ins._wait_ge(self._sem(last[0]), last[1])

    def _record(self, e, reads, writes, name, val):
        for (buf, lo, hi) in writes:
            lst = self.recs.setdefault(buf, [])
            lst[:] = [r for r in lst if not (lo <= r[0] and r[1] <= hi)]
            lst.append([lo, hi, True, name, val, e])
        for (buf, lo, hi) in reads:
            lst = self.recs.setdefault(buf, [])
            lst[:] = [r for r in lst if not ((not r[2]) and r[5] == e and lo <= r[0] and r[1] <= hi)]
            lst.append([lo, hi, False, name, val, e])

    def op(self, e, fn, reads=(), writes=()):
        last = self._wait(e, self._deps(e, reads, writes), keep_last=True)
        ins = fn(self.eng[e])
        self._fuse(ins, last)
        self.cnt[e] += 1
        ins.then_inc(self.esem[e], 1)
        self._record(e, reads, writes, "E:" + e, self.cnt[e])

    def mm_group(self, out_region, mms):
        e = "pe"
        ticket = self.cnt[e] + 1
        bank_wait = self._wait(e, self._deps(e, (), (out_region,)), keep_last=True)
        last = None
        for fn, reads in mms:
            self._wait(e, self._deps(e, reads, ()))
            last = fn(self.eng[e])
            if bank_wait is not None:
                self._fuse(last, bank_wait)
                bank_wait = None
            self._record(e, reads, (), "E:pe", ticket)
        last.then_inc(self.esem[e], 1)
        self.cnt[e] = ticket
        self._record(e, (), (out_region,), "E:pe", ticket)

    def mm_interleaved(self, bank_regions, seq):
        e = "pe"
        opened = set()
        pending = []
        for fn, reads, bi, last in seq:
            if bi not in opened:
                opened.add(bi)
                self._wait(e, self._deps(e, (), (bank_regions[bi],)))
            self._wait(e, self._deps(e, reads, ()))
            ins = fn(self.eng[e])
            if last:
                self.cnt[e] += 1
                ins.then_inc(self.esem[e], 1)
                t = self.cnt[e]
                for r in pending:
                    self._record(e, r, (), "E:pe", t)
                pending = []
                self._record(e, reads, (), "E:pe", t)
                self._record(e, (), (bank_regions[bi],), "E:pe", t)
            else:
                pending.append(reads)
        assert not pending

    def dma(self, e, semname, fns, reads=(), writes=()):
        if not isinstance(fns, (list, tuple)):
            fns = [fns]
        self._wait(e, self._deps(e, reads, writes))
        h = self.dsem[semname]
        for fn in fns:
            fn(self.eng[e]).then_inc(h[0], 16)
            h[1] += 16
        self._record("dma:" + semname, reads, writes, semname, h[1])

    def wait_dma_total(self, e, semname):
        h = self.dsem[semname]
        if self.waited[e].get(semname, 0) < h[1]:
            self.eng[e].wait_ge(h[0], h[1])
            self.waited[e][semname] = h[1]


def build_program(S=SEQ, depth=DEPTH, W=WMAX, nslot=NSLOT, debug=False):
    nc = bass.Bass("TRN2", target_bir_lowering=False)
    dbg_names = []
    if debug:
        dbg_f = nc.dram_tensor("dbg_f", [64, 128, 512], F32, kind="ExternalOutput").ap()
        dbg_b = nc.dram_tensor("dbg_b", [64, 128, 512], BF16, kind="ExternalOutput").ap()
    tiles = [(s0, min(W, S - s0)) for s0 in range(0, S, W)]
    WM = W
    debug_tile = (debug - 1) if debug else -1
    NV = depth * NVL + 8

    xT = nc.dram_tensor("xT", [KD, 128, S], F32, kind="ExternalInput").ap()
    win_f = nc.dram_tensor("win_t", [depth * N_IN * 128, 1024], F32, kind="ExternalInput").ap()
    wup_f = nc.dram_tensor("wup_t", [depth * N_UP * 128, 1024], F32, kind="ExternalInput").ap()
    wout_f = nc.dram_tensor("wout_t", [depth * KD * 128, 1536], F32, kind="ExternalInput").ap()
    wdn_f = nc.dram_tensor("wdn_t", [depth * KD * 128 * 2, 1536], F32, kind="ExternalInput").ap()
    bd_f = nc.dram_tensor("bd_t", [128, depth * 16 * 128], F32, kind="ExternalInput").ap()
    vecs = nc.dram_tensor("vecs", [128, NV], F32, kind="ExternalInput").ap()
    outT = nc.dram_tensor("outT", [KD, 128, S], F32, kind="ExternalOutput").ap()
    win_b = nc.dram_tensor("win_b", [depth * N_IN * 128, 1024], BF16, kind="Internal").ap()
    wup_b = nc.dram_tensor("wup_b", [depth * N_UP * 128, 1024], BF16, kind="Internal").ap()
    wout_b = nc.dram_tensor("wout_b", [depth * KD * 128, 1536], BF16, kind="Internal").ap()
    wdn_b = nc.dram_tensor("wdn_b", [depth * KD * 128 * 2, 1536], BF16, kind="Internal").ap()

    with ExitStack() as es:
        E = es.enter_context
        sb = lambda name, shape, dt: E(nc.sbuf_tensor(name, shape, dt))
        X = [sb("X0", [128, KD, WM], F32), sb("X1", [128, KD, WM], F32)]
        H = sb("H", [128, KD, WM + HALO], BF16)
        HCAR = sb("HCAR", [128, depth * 2, KD, HALO], BF16)
        SQ = sb("SQ", [128, KD, WM], BF16)
        DUM = sb("DUM", [128, 2], F32)
        Y = sb("Y", [128, N_MIX, WM], BF16)
        Q = sb("Q", [128, KD, WM], F32)
        A = sb("A", [128, KD, WM], F32)
        M = sb("M", [128, KD, WM], F32)
        T = sb("T", [128, 4, WM], F32)
        XCB = sb("XCB", [128, 4, WM], BF16)
        TH = sb("TH", [128, 2, WM], F32)
        HS = sb("HS", [128, 2, WM], F32)
        CS = sb("CS", [128, 2, WM + 2], F32)
        CT = sb("CT", [128, 2, WM], F32)
        ACTB = sb("ACTB", [128, N_FF, WM], BF16)
        G = sb("G", [128, 2, WM], F32)
        U = sb("U", [128, 2, WM], F32)
        GE = sb("GE", [128, 2, WM], F32)
        V = sb("V", [128, NV], F32)
        DV = sb("DV", [128, depth * NDV], F32)
        HST = sb("HST", [128, depth * KD], F32)
        ONES = sb("ONES", [128, 128], BF16)
        BD = sb("BD", [128, depth * 16, 128], BF16)
        WR = sb("WR", [128, nslot, 1024], BF16)
        PS = [E(nc.psum_tensor("PS%d" % i, [128, 512], F32)) for i in range(8)]

        sc = Sched(nc, es)
        for i in range(nslot):
            sc.dma_sem("w%d" % i)
        for n in ("x0", "x1", "o0", "o1", "const"):
            sc.dma_sem(n)

        sc.dma_sem("dbg")

        def dump(name, ap, region, n, bf=False):
            if not debug:
                return
            idx = len(dbg_names)
            dbg_names.append((name, bf, n))
            dst = (dbg_b if bf else dbg_f)[idx, :, 0:n]
            sc.dma("sp", "dbg", lambda q: q.dma_start(out=dst, in_=ap), reads=[region])
            sc.wait_dma_total("sp", "dbg")

        def r3(name, L, k, a, b):
            return (name, k * L + a, k * L + b)

        def rX(xi, k, a, b):
            return r3("X%d" % xi, WM, k, a, b)

        def rH(k, a, b):
            return r3("H", WM + HALO, k, a, b)

        rot = {}

        def nxt(name, n=2):
            rot[name] = (rot.get(name, -1) + 1) % n
            return rot[name]

        bank_ctr = [0]

        def new_bank():
            b = bank_ctr[0] % 8
            bank_ctr[0] += 1
            return b

        def rP(b):
            return ("PS%d" % b, 0, 512)

        s0_, w_ = tiles[0]
        sc.dma("sp", "x0",
               lambda q: q.dma_start(out=X[0][:, :, 0:w_], in_=xT[:, :, s0_:s0_ + w_].rearrange("k p s -> p k s")),
               writes=[rX(0, k, 0, w_) for k in range(KD)])
        sc.dma("sp", "const", lambda q: q.dma_start(out=V[:], in_=vecs), writes=[("V", 0, NV)])
        for l in range(depth):
            sc.dma("pool", sc.dma_sem("bd%d" % l),
                   lambda q, l=l: q.dma_start(out=BD[:, l * 16:(l + 1) * 16, :],
                                              in_=bd_f[:, l * 2048:(l + 1) * 2048].rearrange("p (a b) -> p a b", b=128)),
                   writes=[("BD", l * 16 * 128, (l + 1) * 16 * 128)])
        EPSC = sb("EPSC", [128, 2], F32)
        sc.op("dve", lambda v: v.memset(EPSC[:], EPS), writes=[("EPSC", 0, 2)])
        sc.op("dve", lambda v: v.memset(DUM[:], 1.0), writes=[("DUM", 0, 2)])
        sc.op("dve", lambda v: v.memset(ONES[:], 1.0), writes=[("ONES", 0, 128)])
        sc.op("dve", lambda v: v.memset(HST[:], 0.0), writes=[("HST", 0, depth * KD)])
        sc.op("pool", lambda g: g.memset(HCAR[:], 0.0),
              writes=[("HCAR", 0, depth * 2 * KD * HALO)])
        for l in range(depth):
            vb = l * NVL
            db = l * NDV
            lam = V[:, vb + V_LAM:vb + V_LAM + 8]
            tcol = lambda i: DV[:, db + NDV_BASE + 8 * i:db + NDV_BASE + 8 * i + 8]
            rD = ("DV", db, db + NDV)
            rVv = ("V", 0, NV)
            dv = lambda fn, rd=(rD,): sc.op("dve", fn, reads=list(rd), writes=[rD])
            sc.op("act", lambda a: a.activation(out=tcol(0), in_=lam, func=AF.Abs), reads=[rVv], writes=[rD])
            sc.op("act", lambda a: a.activation(out=tcol(1), in_=tcol(0), func=AF.Exp, scale=-1.0),
                  reads=[rD], writes=[rD])
            dv(lambda v: v.tensor_scalar(out=tcol(2), in0=tcol(1), scalar1=2.0, scalar2=None, op0=ALU.add))
            dv(lambda v: v.reciprocal(out=tcol(2), in_=tcol(2)))
            dv(lambda v: v.tensor_tensor(out=tcol(2), in0=tcol(2), in1=tcol(1), op=ALU.mult))
            dv(lambda v: v.tensor_tensor(out=tcol(3), in0=tcol(2), in1=tcol(2), op=ALU.mult))
            dv(lambda v: v.tensor_scalar(out=tcol(4), in0=tcol(3), scalar1=1.0 / 11.0, scalar2=1.0 / 9.0,
                                         op0=ALU.mult, op1=ALU.add))
            for cst in (1.0 / 7.0, 1.0 / 5.0, 1.0 / 3.0, 1.0):
                dv(lambda v: v.tensor_tensor(out=tcol(4), in0=tcol(4), in1=tcol(3), op=ALU.mult))
                dv(lambda v, cst=cst: v.tensor_scalar(out=tcol(4), in0=tcol(4), scalar1=cst, scalar2=None, op0=ALU.add))
            dv(lambda v: v.tensor_tensor(out=tcol(4), in0=tcol(4), in1=tcol(2), op=ALU.mult))
            dv(lambda v: v.tensor_scalar(out=tcol(0), in0=lam, scalar1=-1.0, scalar2=0.0, op0=ALU.mult, op1=ALU.max),
               rd=(rVv, rD))
            dv(lambda v: v.scalar_tensor_tensor(out=tcol(0), in0=tcol(4), scalar=2.0, in1=tcol(0),
                                                op0=ALU.mult, op1=ALU.add))
            dv(lambda v: v.tensor_scalar(out=DV[:, db + DV_C:db + DV_C + 8], in0=tcol(0), scalar1=-8.0, scalar2=None,
                                         op0=ALU.mult))
            dv(lambda v: v.tensor_scalar(out=DV[:, db + DV_HC:db + DV_HC + 8], in0=tcol(0), scalar1=-4.0, scalar2=None,
                                         op0=ALU.mult))
            sc.op("dve", lambda v, vb=vb, db=db: v.tensor_scalar(
                out=DV[:, db + DV_HBA:db + DV_HBA + 8], in0=V[:, vb + V_BA:vb + V_BA + 8],
                scalar1=0.5, scalar2=None, op0=ALU.mult),
                reads=[("V", 0, NV)], writes=[("DV", db + DV_HBA, db + DV_HBA + 8)])
            sc.op("dve", lambda v, vb=vb, db=db: v.tensor_scalar(
                out=DV[:, db + DV_HBX:db + DV_HBX + 8], in0=V[:, vb + V_BX:vb + V_BX + 8],
                scalar1=0.5, scalar2=None, op0=ALU.mult),
                reads=[("V", 0, NV)], writes=[("DV", db + DV_HBX, db + DV_HBX + 8)])

        dump("DV", DV[:, 0:NDV], ("DV", 0, NDV), NDV)

        def weight_order():
            for (s0, w) in tiles:
                for l in range(depth):
                    for j in range(8 + SK):
                        if j < 8:
                            yield ("in", l, j, 0)
                        if j in SC_AT:
                            for base in (16, 20, 24):
                                yield ("in", l, base + SC_AT[j], 0)
                    for j in range(8):
                        yield ("in", l, 8 + j, 0)
                    for m in range(8):
                        yield ("out", l, m, 0)
                        yield ("out", l, m, 1)
                    for j in range(N_FF):
                        yield ("up", l, j, 0)
                        yield ("up", l, N_FF + j, 0)
                    for m in range(8):
                        for part in range(3):
                            yield ("dn", l, m, part)

        SK = 3
        SC_AT = {2: 0, 4: 1, 6: 2, 8: 3}
        worder = list(weight_order())
        wstate = {"issued": 0, "pos": 0}

        F32T = {"win": win_f, "wup": wup_f, "wout": wout_f, "wdn": wdn_f}
        B16T = {"win": win_b, "wup": wup_b, "wout": wout_b, "wdn": wdn_b}

        def w_pieces(key):
            kind, l, idx, part = key
            if kind == "in":
                return [("win", (l * N_IN + idx) * 128, 0, 1024, 0)]
            if kind == "up":
                return [("wup", (l * N_UP + idx) * 128, 0, 1024, 0)]
            if kind == "out":
                r = (l * KD + idx) * 128
                return [("wout", r, 0, 1024, 0)] if part == 0 else [("wout", r, 1024, 512, 0)]
            r = (l * KD + idx) * 256
            if part == 0:
                return [("wdn", r, 0, 1024, 0)]
            if part == 1:
                return [("wdn", r, 1024, 512, 0), ("wdn", r + 128, 0, 512, 512)]
            return [("wdn", r + 128, 512, 1024, 0)]

        def dreg(pc):
            name, r, c0, n, d = pc
            return ("%s_b:%d" % (name, c0), r, r + 128)

        def issue_load(i):
            slot = i % nslot
            pcs = w_pieces(worder[i])
            ntot = sum(pc[3] for pc in pcs)
            sc.dma("sp", "w%d" % slot,
                   [lambda q, pc=pc: q.dma_start(out=WR[:, slot, pc[4]:pc[4] + pc[3]],
                                                 in_=B16T[pc[0]][pc[1]:pc[1] + 128, pc[2]:pc[2] + pc[3]]) for pc in pcs],
                   reads=[dreg(pc) for pc in pcs],
                   writes=[("WR", slot * 1024, slot * 1024 + ntot)])

        T0N = len(worder) // len(tiles)
        NSTG = 4
        CA = 2
        STG = sb("STG", [128, NSTG, 1024], F32)
        for i_ in range(NSTG):
            sc.dma_sem("stg%d" % i_)
        for i_ in range(nslot):
            sc.dma_sem("wb%d" % i_)
        t0 = {"dma": 0, "cast": 0}

        def t0_dma(c):
            st = c % NSTG
            pcs = w_pieces(worder[c])
            ntot = sum(pc[3] for pc in pcs)
            sc.dma("sp", "stg%d" % st,
                   [lambda q, pc=pc: q.dma_start(out=STG[:, st, pc[4]:pc[4] + pc[3]],
                                                 in_=F32T[pc[0]][pc[1]:pc[1] + 128, pc[2]:pc[2] + pc[3]]) for pc in pcs],
                   writes=[("STG", st * 1024, st * 1024 + ntot)])

        def t0_cast(c):
            st = c % NSTG
            slot = c % nslot
            pcs = w_pieces(worder[c])
            ntot = sum(pc[3] for pc in pcs)
            sc.op("act", lambda a: a.activation(out=WR[:, slot, 0:ntot], in_=STG[:, st, 0:ntot], func=AF.Copy),
                  reads=[("STG", st * 1024, st * 1024 + ntot)], writes=[("WR", slot * 1024, slot * 1024 + ntot)])
            if len(tiles) > 1:
                sc.dma("sp", "wb%d" % slot,
                       [lambda q, pc=pc: q.dma_start(out=B16T[pc[0]][pc[1]:pc[1] + 128, pc[2]:pc[2] + pc[3]],
                                                     in_=WR[:, slot, pc[4]:pc[4] + pc[3]]) for pc in pcs],
                       reads=[("WR", slot * 1024, slot * 1024 + ntot)],
                       writes=[dreg(pc) for pc in pcs])

        def t0_advance(i):
            target = min(T0N, i + CA + 1)
            while t0["cast"] < target:
                while t0["dma"] <= t0["cast"]:
                    t0_dma(t0["dma"])
                    t0["dma"] += 1
                t0_cast(t0["cast"])
                t0["cast"] += 1
                while t0["dma"] < min(T0N, t0["cast"] + NSTG):
                    t0_dma(t0["dma"])
                    t0["dma"] += 1

        wstate["issued"] = T0N

        def wget(key, q=0):
            i = wstate["pos"]
            assert worder[i] == key, (worder[i], key)
            wstate["pos"] += 1
            if i < T0N:
                t0_advance(i)
            while wstate["issued"] < min(len(worder), i - q + nslot):
                issue_load(wstate["issued"])
                wstate["issued"] += 1
            return i % nslot

        def wap(slot, kk):
            return WR[:, slot, kk * 128:(kk + 1) * 128]

        def rW(slot, kk):
            return ("WR", slot * 1024 + kk * 128, slot * 1024 + (kk + 1) * 128)

        def load_x(ti):
            s0, w = tiles[ti]
            xi = ti % 2
            sc.dma("sp", "x%d" % xi,
                   lambda q: q.dma_start(out=X[xi][:, :, 0:w], in_=xT[:, :, s0:s0 + w].rearrange("k p s -> p k s")),
                   writes=[rX(xi, k, 0, w) for k in range(KD)])

        def rms_stats(xi, w):
            Xt = X[xi]
            b = new_bank()
            mms = []
            sc.op("act", lambda a: a.activation(out=DUM[:, 1:2], in_=DUM[:, 0:1], func=AF.Ln),
                  reads=[("DUM", 0, 1)], writes=[("DUM", 1, 2)])
            for k in range(KD):
                sc.op("act", lambda a, k=k: a.activation(out=SQ[:, k, 0:w], in_=Xt[:, k, 0:w], func=AF.Square),
                      reads=[rX(xi, k, 0, w)], writes=[r3("SQ", WM, k, 0, w)])
            for k in range(KD):
                mms.append((lambda pe, k=k: pe.matmul(PS[b][:, 0:w], lhsT=ONES[:], rhs=SQ[:, k, 0:w],
                                                      start=(k == 0), stop=(k == KD - 1)),
                            [("ONES", 0, 128), r3("SQ", WM, k, 0, w)]))
            sc.mm_group(rP(b), mms)
            sc.op("act", lambda a: a.activation(out=PS[b][:, 0:w], in_=PS[b][:, 0:w], func=AF.Ln,
                                                scale=1.0 / D, bias=EPSC[:, 0:1]),
                  reads=[rP(b), ("EPSC", 0, 1)], writes=[rP(b)])
            sc.op("act", lambda a: a.activation(out=PS[b][:, 0:w], in_=PS[b][:, 0:w], func=AF.Exp, scale=-0.5),
                  reads=[rP(b)], writes=[rP(b)])
            return b

        def norm_to_h(xi, w, gcol, car):
            Xt = X[xi]
            rb = rms_stats(xi, w)
            sc.op("pool", lambda g: g.tensor_copy(out=H[:, :, 0:HALO], in_=HCAR[:, car]),
                  reads=[("HCAR", car * KD * HALO, (car + 1) * KD * HALO)],
                  writes=[rH(k, 0, HALO) for k in range(KD)])
            for k in range(KD):
                sc.op("dve", lambda v, k=k: v.scalar_tensor_tensor(
                    out=H[:, k, HALO:HALO + w], in0=Xt[:, k, 0:w], scalar=V[:, gcol + k:gcol + k + 1],
                    in1=PS[rb][:, 0:w], op0=ALU.mult, op1=ALU.mult),
                    reads=[rX(xi, k, 0, w), ("V", 0, NV), rP(rb)], writes=[rH(k, HALO, HALO + w)])
            sc.op("pool", lambda g: g.tensor_copy(out=HCAR[:, car], in_=H[:, :, w:w + HALO]),
                  reads=[rH(k, w, w + HALO) for k in range(KD)],
                  writes=[("HCAR", car * KD * HALO, (car + 1) * KD * HALO)])

        def proj_group(key, b, c0, n):
            slot = wget(key)
            mms = []
            for k in range(KD):
                mms.append((lambda pe, k=k: pe.matmul(PS[b][:, 0:n], lhsT=wap(slot, k), rhs=H[:, k, c0:c0 + n],
                                                      start=(k == 0), stop=(k == KD - 1)),
                            [rW(slot, k), rH(k, c0, c0 + n)]))
            sc.mm_group(rP(b), mms)

        def proj_groups_kmajor(keys, banks, c0, n):
            slots = [wget(key, q=qi) for qi, key in enumerate(keys)]
            seq = []
            for k in range(KD):
                for gi in range(len(keys)):
                    seq.append((lambda pe, k=k, gi=gi: pe.matmul(
                        PS[banks[gi]][:, 0:n], lhsT=wap(slots[gi], k), rhs=H[:, k, c0:c0 + n],
                        start=(k == 0), stop=(k == KD - 1)),
                        [rW(slots[gi], k), rH(k, c0, c0 + n)], gi, k == KD - 1))
            sc.mm_interleaved([rP(b) for b in banks], seq)

        cur = {"ti": 0, "l": 0}

        def dbg_on():
            return debug and cur["ti"] == debug_tile and cur["l"] == 0

        def mixer(l, xi, w):
            vb = l * NVL
            db = l * NDV
            n3 = w + HALO
            vcol = lambda c: V[:, vb + c:vb + c + 1]
            dcol = lambda c: DV[:, db + c:db + c + 1]
            rV = ("V", 0, NV)
            rDV = ("DV", db, db + NDV)

            stA = {}

            NPRE = 3
            pre_banks = [new_bank() for _ in range(NPRE)]
            proj_groups_kmajor([("in", l, j, 0) for j in range(NPRE)], pre_banks, 0, n3)

            def lru_a_front(j):
                if j < NPRE:
                    b = pre_banks[j]
                else:
                    b = new_bank()
                    proj_group(("in", l, j, 0), b, 0, n3)
                t = nxt("T", 4)
                sc.op("act", lambda a: a.activation(out=T[:, t, 0:w], in_=PS[b][:, 3:3 + w], func=AF.Identity,
                                                    scale=vcol(V_CW + 3 * 8 + j), bias=vcol(V_CB + j)),
                      reads=[rP(b), rV], writes=[r3("T", WM, t, 0, w)])
                for tap in (2, 1, 0):
                    sc.op("dve", lambda v, tap=tap: v.scalar_tensor_tensor(
                        out=T[:, t, 0:w], in0=PS[b][:, tap:tap + w], scalar=vcol(V_CW + tap * 8 + j),
                        in1=T[:, t, 0:w], op0=ALU.mult, op1=ALU.add),
                        reads=[rP(b), rV, r3("T", WM, t, 0, w)], writes=[r3("T", WM, t, 0, w)])
                stA[j] = t

            def lru_a_cast(j):
                t = stA[j]
                sc.op("act", lambda a: a.activation(out=XCB[:, t, 0:w], in_=T[:, t, 0:w], func=AF.Copy),
                      reads=[r3("T", WM, t, 0, w)], writes=[r3("XCB", WM, t, 0, w)])

            stB = {}

            def lru_a_back_pe(j):
                t = stA[j]
                ba_, bi_ = new_bank(), new_bank()
                stB[j] = (ba_, bi_)
                for (bb, which) in ((ba_, 0), (bi_, 1)):
                    sc.mm_group(rP(bb), [(lambda pe, bb=bb, which=which: pe.matmul(
                        PS[bb][:, 0:w], lhsT=BD[:, l * 16 + 2 * j + which, :], rhs=XCB[:, t, 0:w], start=True, stop=True),
                        [("BD", (l * 16 + 2 * j + which) * 128, (l * 16 + 2 * j + which + 1) * 128),
                         r3("XCB", WM, t, 0, w)])])

            def lru_a_back(j):
                t = stA.pop(j)
                ba_, bi_ = stB.pop(j)
                h = nxt("TH")
                sc.op("act", lambda a: a.activation(out=TH[:, h, 0:w], in_=PS[ba_][:, 0:w], func=AF.Tanh,
                                                    scale=0.5, bias=dcol(DV_HBA + j)),
                      reads=[rP(ba_), rDV], writes=[r3("TH", WM, h, 0, w)])
                sc.op("act", lambda a: a.activation(out=A[:, j, 0:w], in_=TH[:, h, 0:w], func=AF.Exp,
                                                    scale=dcol(DV_HC + j), bias=dcol(DV_HC + j)),
                      reads=[r3("TH", WM, h, 0, w), rDV], writes=[r3("A", WM, j, 0, w)])
                sc.op("pool", lambda g: g.tensor_tensor(out=M[:, j, 0:w], in0=A[:, j, 0:w], in1=A[:, j, 0:w], op=ALU.mult),
                      reads=[r3("A", WM, j, 0, w)], writes=[r3("M", WM, j, 0, w)])
                sc.op("pool", lambda g: g.tensor_scalar(out=M[:, j, 0:w], in0=M[:, j, 0:w], scalar1=-1.0, scalar2=1.0,
                                                        op0=ALU.mult, op1=ALU.add),
                      reads=[r3("M", WM, j, 0, w)], writes=[r3("M", WM, j, 0, w)])
                sc.op("pool", lambda g: g.tensor_scalar(out=M[:, j, 0:w], in0=M[:, j, 0:w], scalar1=2.0, scalar2=0.0,
                                                        op0=ALU.min, op1=ALU.max),
                      reads=[r3("M", WM, j, 0, w)], writes=[r3("M", WM, j, 0, w)])
                sc.op("act", lambda a: a.activation(out=PS[bi_][:, 0:w], in_=PS[bi_][:, 0:w], func=AF.Tanh,
                                                    scale=0.5, bias=dcol(DV_HBX + j)),
                      reads=[rP(bi_), rDV], writes=[rP(bi_)])
                sc.op("dve", lambda v: v.scalar_tensor_tensor(
                    out=Q[:, j, 0:w], in0=PS[bi_][:, 0:w], scalar=1.0, in1=T[:, t, 0:w], op0=ALU.add, op1=ALU.mult),
                    reads=[rP(bi_), r3("T", WM, t, 0, w)], writes=[r3("Q", WM, j, 0, w)])

            def sc_chunk(j):
                bb, bc, bx = new_bank(), new_bank(), new_bank()
                proj_group(("in", l, 16 + j, 0), bb, 0, n3)
                proj_group(("in", l, 20 + j, 0), bc, 0, n3)
                proj_group(("in", l, 24 + j, 0), bx, 0, n3)
                c = nxt("CS")
                sc.op("act", lambda a: a.activation(out=CS[:, c, 0:w + 2], in_=PS[bc][:, 1:w + 3], func=AF.Copy),
                      reads=[rP(bc)], writes=[r3("CS", WM + 2, c, 0, w + 2)])
                sc.op("dve", lambda v: v.tensor_tensor(out=PS[bx][:, 1:w + 3], in0=PS[bx][:, 1:w + 3],
                                                       in1=CS[:, c, 0:w + 2], op=ALU.mult),
                      reads=[rP(bx), r3("CS", WM + 2, c, 0, w + 2)], writes=[rP(bx)])
                sc.op("act", lambda a: a.activation(out=CT[:, c, 0:w], in_=PS[bx][:, 3:3 + w], func=AF.Identity,
                                                    scale=vcol(V_SCW + 2 * 4 + j)),
                      reads=[rP(bx), rV], writes=[r3("CT", WM, c, 0, w)])
                for tap in (1, 0):
                    sc.op("dve", lambda v, tap=tap: v.scalar_tensor_tensor(
                        out=CT[:, c, 0:w], in0=PS[bx][:, 1 + tap:1 + tap + w], scalar=vcol(V_SCW + tap * 4 + j),
                        in1=CT[:, c, 0:w], op0=ALU.mult, op1=ALU.add),
                        reads=[rP(bx), rV, r3("CT", WM, c, 0, w)],
                        writes=[r3("CT", WM, c, 0, w)])
                sc.op("dve", lambda v: v.tensor_tensor(out=Y[:, 8 + j, 0:w], in0=PS[bb][:, 3:3 + w],
                                                       in1=CT[:, c, 0:w], op=ALU.mult),
                      reads=[rP(bb), r3("CT", WM, c, 0, w)], writes=[r3("Y", WM, 8 + j, 0, w)])


            for j in range(8 + SK):
                if j < 8:
                    lru_a_front(j)
                if 0 <= j - 1 < 8:
                    lru_a_cast(j - 1)
                if 0 <= j - SK < 8:
                    lru_a_back_pe(j - SK)
                    lru_a_back(j - SK)
                if j in SC_AT:
                    sc_chunk(SC_AT[j])

            rM = [r3("M", WM, k, 0, w) for k in range(KD)]
            for pc in range(4):
                sc.op("act", lambda a, pc=pc: a.activation(out=M[:, 2 * pc:2 * pc + 2, 0:w], in_=M[:, 2 * pc:2 * pc + 2, 0:w],
                                                           func=AF.Sqrt, scale=1.0, bias=1e-30),
                      reads=rM[2 * pc:2 * pc + 2], writes=rM[2 * pc:2 * pc + 2])
            for j in range(8):
                sc.op("pool", lambda g, j=j: g.tensor_tensor(out=Q[:, j, 0:w], in0=Q[:, j, 0:w], in1=M[:, j, 0:w], op=ALU.mult),
                      reads=[r3("Q", WM, j, 0, w), r3("M", WM, j, 0, w)], writes=[r3("Q", WM, j, 0, w)])
            if dbg_on():
                dump("A0", A[:, 0, 0:w], r3("A", WM, 0, 0, w), w)
                dump("M0", M[:, 0, 0:w], r3("M", WM, 0, 0, w), w)
                dump("Q0", Q[:, 0, 0:w], r3("Q", WM, 0, 0, w), w)
                dump("Y8", Y[:, 8, 0:w], r3("Y", WM, 8, 0, w), w, bf=True)

            for j in range(8):
                b = new_bank()
                proj_group(("in", l, 8 + j, 0), b, HALO, w)
                sc.op("act", lambda a: a.activation(out=PS[b][:, 0:w], in_=PS[b][:, 0:w], func=AF.Gelu_apprx_tanh),
                      reads=[rP(b)], writes=[rP(b)])
                hs = nxt("HS")
                sidx = l * KD + j
                sc.op("dve", lambda v: v.tensor_tensor_scan(
                    out=HS[:, hs, 0:w], data0=A[:, j, 0:w], data1=Q[:, j, 0:w], initial=HST[:, sidx:sidx + 1],
                    op0=ALU.mult, op1=ALU.add),
                    reads=[r3("A", WM, j, 0, w), r3("Q", WM, j, 0, w), ("HST", sidx, sidx + 1)],
                    writes=[r3("HS", WM, hs, 0, w)])
                sc.op("dve", lambda v: v.tensor_copy(out=HST[:, sidx:sidx + 1], in_=HS[:, hs, w - 1:w]),
                      reads=[r3("HS", WM, hs, w - 1, w)], writes=[("HST", sidx, sidx + 1)])
                sc.op("dve", lambda v: v.scalar_tensor_tensor(
                    out=Y[:, j, 0:w], in0=HS[:, hs, 0:w], scalar=0.5, in1=PS[b][:, 0:w], op0=ALU.mult, op1=ALU.mult),
                    reads=[r3("HS", WM, hs, 0, w), rP(b)], writes=[r3("Y", WM, j, 0, w)])

        def residual_proj(kind, l, xi, w, src_name, SRC, nk):
            Xt = X[xi]
            nparts = (nk + 7) // 8
            for m in range(8):
                b = new_bank()
                mms = []
                for part in range(nparts):
                    slot = wget((kind, l, m, part), q=part)
                    if dbg_on() and m == 0:
                        dump("W%s%d" % (kind, part), WR[:, slot, 0:512], ("WR", slot * 1024, slot * 1024 + 512), 512, bf=True)
                    for kk in range(min(8, nk - part * 8)):
                        k = part * 8 + kk
                        mms.append((lambda pe, slot=slot, kk=kk, k=k: pe.matmul(
                            PS[b][:, 0:w], lhsT=wap(slot, kk), rhs=SRC[:, k, 0:w], start=(k == 0), stop=(k == nk - 1)),
                            [rW(slot, kk), r3(src_name, WM, k, 0, w)]))
                sc.mm_group(rP(b), mms)
                sc.op("dve", lambda v, m=m: v.tensor_tensor(out=Xt[:, m, 0:w], in0=PS[b][:, 0:w], in1=Xt[:, m, 0:w], op=ALU.add),
                      reads=[rP(b), rX(xi, m, 0, w)], writes=[rX(xi, m, 0, w)])

        def out_proj(l, xi, w):
            Xt = X[xi]
            korder = [8, 9, 10, 11, 0, 1, 2, 3, 4, 5, 6, 7]
            for half in range(2):
                ms = [half * 4 + i for i in range(4)]
                banks = [new_bank() for _ in ms]
                slots = {}
                q = 0
                for m in ms:
                    for part in range(2):
                        slots[(m, part)] = wget(("out", l, m, part), q=q)
                        q += 1
                seq = []
                if half == 0:
                    order = [(ki, k, i) for i in range(len(ms)) for ki, k in enumerate(korder[:4])]
                    order += [(ki + 4, k, i) for ki, k in enumerate(korder[4:]) for i in range(len(ms))]
                else:
                    order = [(ki, k, i) for i in range(len(ms)) for ki, k in enumerate(korder)]
                for ki, k, i in order:
                    m = ms[i]
                    slot = slots[(m, k // 8)]
                    kk = k % 8
                    seq.append((lambda pe, b=banks[i], slot=slot, kk=kk, k=k, ki=ki: pe.matmul(
                        PS[b][:, 0:w], lhsT=wap(slot, kk), rhs=Y[:, k, 0:w],
                        start=(ki == 0), stop=(ki == len(korder) - 1)),
                        [rW(slot, kk), r3("Y", WM, k, 0, w)], i, ki == len(korder) - 1))
                sc.mm_interleaved([rP(b) for b in banks], seq)
                for i, m in enumerate(ms):
                    sc.op("dve", lambda v, m=m, b=banks[i]: v.tensor_tensor(
                        out=Xt[:, m, 0:w], in0=PS[b][:, 0:w], in1=Xt[:, m, 0:w], op=ALU.add),
                        reads=[rP(banks[i]), rX(xi, m, 0, w)], writes=[rX(xi, m, 0, w)])

        def ffn(l, xi, w, hook=None):
            vb = l * NVL
            vcol = lambda c: V[:, vb + c:vb + c + 1]
            rV = ("V", 0, NV)
            n2 = w + 2
            pre = [new_bank() for _ in range(4)]
            proj_groups_kmajor([("up", l, 0, 0), ("up", l, N_FF, 0), ("up", l, 1, 0), ("up", l, N_FF + 1, 0)],
                               pre, 1, n2)
            for j in range(N_FF):
                if j < 2:
                    bg, bu = pre[2 * j], pre[2 * j + 1]
                else:
                    bg, bu = new_bank(), new_bank()
                    proj_group(("up", l, j, 0), bg, 1, n2)
                    proj_group(("up", l, N_FF + j, 0), bu, 1, n2)
                r = nxt("G")
                for (bb, BUF, name, ch) in ((bg, G, "G", j), (bu, U, "U", N_FF + j)):
                    sc.op("act", lambda a, bb=bb, BUF=BUF, ch=ch: a.activation(
                        out=BUF[:, r, 0:w], in_=PS[bb][:, 2:2 + w], func=AF.Identity, scale=vcol(V_FCW + 2 * N_UP + ch)),
                        reads=[rP(bb), rV], writes=[r3(name, WM, r, 0, w)])
                    for tap in (1, 0):
                        sc.op("dve", lambda v, bb=bb, BUF=BUF, ch=ch, tap=tap: v.scalar_tensor_tensor(
                            out=BUF[:, r, 0:w], in0=PS[bb][:, tap:tap + w], scalar=vcol(V_FCW + tap * N_UP + ch),
                            in1=BUF[:, r, 0:w], op0=ALU.mult, op1=ALU.add),
                            reads=[rP(bb), rV, r3(name, WM, r, 0, w)], writes=[r3(name, WM, r, 0, w)])
                sc.op("act", lambda a: a.activation(out=GE[:, r, 0:w], in_=G[:, r, 0:w], func=AF.Gelu_apprx_tanh),
                      reads=[r3("G", WM, r, 0, w)], writes=[r3("GE", WM, r, 0, w)])
                sc.op("pool", lambda g: g.tensor_tensor(out=ACTB[:, j, 0:w], in0=GE[:, r, 0:w], in1=U[:, r, 0:w], op=ALU.mult),
                      reads=[r3("GE", WM, r, 0, w), r3("U", WM, r, 0, w)], writes=[r3("ACTB", WM, j, 0, w)])
                if hook is not None and j == 3:
                    hook()

        def final_norm_store(ti):
            s0, w = tiles[ti]
            xi = ti % 2
            Xt = X[xi]
            gcol = depth * NVL
            rb = rms_stats(xi, w)
            for k in range(KD):
                sc.op("dve", lambda v, k=k: v.scalar_tensor_tensor(
                    out=Xt[:, k, 0:w], in0=Xt[:, k, 0:w], scalar=V[:, gcol + k:gcol + k + 1],
                    in1=PS[rb][:, 0:w], op0=ALU.mult, op1=ALU.mult),
                    reads=[rX(xi, k, 0, w), ("V", 0, NV), rP(rb)], writes=[rX(xi, k, 0, w)])
                if ti == len(tiles) - 1:
                    sc.dma("sp", "o%d" % xi,
                           lambda q, k=k: q.dma_start(out=outT[k, :, s0:s0 + w], in_=Xt[:, k, 0:w]),
                           reads=[rX(xi, k, 0, w)])
            if ti == len(tiles) - 1:
                return
            sc.dma("sp", "o%d" % xi,
                   lambda q: q.dma_start(out=outT[:, :, s0:s0 + w].rearrange("k p s -> p k s"), in_=Xt[:, :, 0:w]),
                   reads=[rX(xi, k, 0, w) for k in range(KD)])

        for ti, (s0, w) in enumerate(tiles):
            xi = ti % 2
            for l in range(depth):
                cur["ti"], cur["l"] = ti, l
                if l == depth - 1 and ti + 1 < len(tiles):
                    load_x(ti + 1)
                norm_to_h(xi, w, l * NVL + V_G1, l * 2 + 0)
                if dbg_on():
                    dump("X_in", X[xi][:, 0, 0:w], rX(xi, 0, 0, w), w)
                    dump("H1", H[:, 0, 0:w + HALO], rH(0, 0, w + HALO), w + HALO, bf=True)
                mixer(l, xi, w)
                if dbg_on():
                    for jj in range(1, 8):
                        dump("Y%d" % jj, Y[:, jj, 0:w], r3("Y", WM, jj, 0, w), w, bf=True)
                    dump("Y0", Y[:, 0, 0:w], r3("Y", WM, 0, 0, w), w, bf=True)
                    dump("Y11", Y[:, 11, 0:w], r3("Y", WM, 11, 0, w), w, bf=True)
                out_proj(l, xi, w)
                if dbg_on():
                    dump("X_mid", X[xi][:, 0, 0:w], rX(xi, 0, 0, w), w)
                norm_to_h(xi, w, l * NVL + V_G2, l * 2 + 1)
                ffn(l, xi, w, hook=(lambda ti=ti: final_norm_store(ti - 1)) if (l == 0 and ti > 0) else None)
                if dbg_on():
                    dump("ACT0", ACTB[:, 0, 0:w], r3("ACTB", WM, 0, 0, w), w, bf=True)
                    dump("ACT23", ACTB[:, 23, 0:w], r3("ACTB", WM, 23, 0, w), w, bf=True)
                residual_proj("dn", l, xi, w, "ACTB", ACTB, N_FF)
                if dbg_on():
                    dump("X_out", X[xi][:, 0, 0:w], rX(xi, 0, 0, w), w)
        final_norm_store(len(tiles) - 1)
        assert wstate["pos"] == len(worder)
        for n in ("o0", "o1"):
            sc.wait_dma_total("sp", n)
        build_program.stats = dict(cnt=dict(sc.cnt), nwaits=sc.nwaits, banks=bank_ctr[0])
        build_program.dbg_names = dbg_names
    return nc


def prepare_weights(inputs, depth=DEPTH):
    f = lambda a: np.ascontiguousarray(np.asarray(a, dtype=np.float32))
    w_in = f(inputs["w_in"])[:depth]
    w_up = f(inputs["w_up"])[:depth]
    w_out = f(inputs["w_out"])[:depth]
    w_dn = f(inputs["w_down"])[:depth]
    L = depth
    win_t = w_in.reshape(L, KD, 128, N_IN, 128).transpose(0, 3, 2, 1, 4).reshape(L * N_IN * 128, 1024)
    wup_t = w_up.reshape(L, KD, 128, N_UP, 128).transpose(0, 3, 2, 1, 4).reshape(L * N_UP * 128, 1024)
    wout_t = w_out.reshape(L, N_MIX, 128, KD, 128).transpose(0, 3, 2, 1, 4).reshape(L * KD * 128, N_MIX * 128)
    wdn_t = w_dn.reshape(L, N_FF, 128, KD, 128).transpose(0, 3, 2, 1, 4).reshape(L, KD, 128, 2, 1536)
    wdn_t = wdn_t.transpose(0, 1, 3, 2, 4).reshape(L * KD * 2 * 128, 1536)
    wa = f(inputs["lru_wa"])[:depth]
    wx = f(inputs["lru_wx"])[:depth]
    bd = np.zeros((128, L, 16, 128), np.float32)
    for l in range(L):
        for c in range(8):
            for which, src in ((0, wa), (1, wx)):
                bd[0:64, l, 2 * c + which, 0:64] = src[l, 2 * c]
                bd[64:128, l, 2 * c + which, 64:128] = src[l, 2 * c + 1]
    bd = bd.reshape(128, L * 16 * 128)
    NV = L * NVL + 8
    vecs = np.zeros((128, NV), np.float32)
    col = lambda v: np.asarray(v, np.float32).reshape(-1, 128).T
    for l in range(L):
        b = l * NVL
        vecs[:, b + V_G1:b + V_G1 + 8] = col(inputs["norm1_g"][l])
        for tap in range(4):
            vecs[:, b + V_CW + tap * 8:b + V_CW + tap * 8 + 8] = col(inputs["lru_conv_w"][l][tap])
        vecs[:, b + V_CB:b + V_CB + 8] = col(inputs["lru_conv_b"][l])
        vecs[:, b + V_BA:b + V_BA + 8] = col(inputs["lru_ba"][l])
        vecs[:, b + V_BX:b + V_BX + 8] = col(inputs["lru_bx"][l])
        vecs[:, b + V_LAM:b + V_LAM + 8] = col(inputs["lru_lambda"][l])
        for tap in range(3):
            vecs[:, b + V_SCW + tap * 4:b + V_SCW + tap * 4 + 4] = col(inputs["sc_conv_w"][l][tap])
        vecs[:, b + V_G2:b + V_G2 + 8] = col(inputs["norm2_g"][l])
        for tap in range(3):
            vecs[:, b + V_FCW + tap * N_UP:b + V_FCW + (tap + 1) * N_UP] = col(inputs["ffn_conv_w"][l][tap])
    vecs[:, L * NVL:L * NVL + 8] = col(inputs["final_g"])
    return dict(win_t=np.ascontiguousarray(win_t), wup_t=np.ascontiguousarray(wup_t),
                wout_t=np.ascontiguousarray(wout_t), wdn_t=np.ascontiguousarray(wdn_t),
                bd_t=np.ascontiguousarray(bd), vecs=vecs)


_PROGRAM_CACHE = {}


def run(inputs, S=SEQ, depth=DEPTH, ncores=NCORES, W=WMAX, debug=False):
    x = np.asarray(inputs["x"], dtype=np.float32)
    assert x.shape[0] == ncores and x.shape[1] == S
    wts = prepare_weights(inputs, depth)
    key = (S, depth, W, debug)
    if key not in _PROGRAM_CACHE:
        _PROGRAM_CACHE[key] = build_program(S=S, depth=depth, W=W, debug=debug)
    nc = _PROGRAM_CACHE[key]
    in_maps = []
    for b in range(ncores):
        xT = np.ascontiguousarray(x[b].T).reshape(KD, 128, S)
        m = dict(wts)
        m["xT"] = xT
        in_maps.append(m)
    res = run_bass_kernel_spmd(nc, in_maps, core_ids=list(range(ncores)))
    out = np.empty((ncores, S, D), np.float32)
    for b in range(ncores):
        out[b] = np.asarray(res.results[b]["outT"], dtype=np.float32).reshape(D, S).T
    if debug:
        run.dbg = [{n: np.asarray(res.results[b]["dbg_b" if bf else "dbg_f"][i], dtype=np.float32)[:, :cnt]
                    for i, (n, bf, cnt) in enumerate(build_program.dbg_names)} for b in range(ncores)]
    return out


def kernel(**inputs):
    return run(inputs)
```
